# Optimizing a Trainium2 kernel written in Bass

```python
import math
import jax, jax.numpy as jnp
from jax import lax
import numpy as np

D_MODEL = 2048
BATCH = 16
SEQ = 2048
DEPTH = 1

MEM_LEN = 256
D_FF = 5632
POOL_GROUPS = 4
POOL_GROUP_DIM = 128
POOL_WIDTH = POOL_GROUPS * POOL_GROUP_DIM
POOL_WINDOWS = (2, 4, 8, 16)
FOX_HEADS = 16
FOX_HEAD_DIM = 64
FOX_WIDTH = FOX_HEADS * FOX_HEAD_DIM
MEM_HEADS = 4
MEM_HEAD_DIM = 128
MEM_WIDTH = MEM_HEADS * MEM_HEAD_DIM
N_BRANCHES = 3
GATE_WIDTH = N_BRANCHES * D_MODEL
Q_BLOCK = 128
EPS = 1e-6
IN_SPLITS = (POOL_WIDTH, FOX_WIDTH, FOX_WIDTH, FOX_WIDTH, FOX_HEADS, MEM_WIDTH, GATE_WIDTH)
IN_WIDTH = sum(IN_SPLITS)

kernel_name = "hybrid_pool_fox_memxattn_macaron"


def rmsnorm(x, g):
    xf = x.astype(jnp.float32)
    y = xf * lax.rsqrt(jnp.mean(xf * xf, axis=-1, keepdims=True) + EPS)
    return (y * g.astype(jnp.float32)).astype(x.dtype)


def swiglu_half_ffn(x, norm_g, w_gate_up, w_down):
    h = rmsnorm(x, norm_g)
    gate, up = jnp.split(h @ w_gate_up, 2, axis=-1)
    return 0.5 * ((jax.nn.silu(gate) * up) @ w_down)


def causal_window_mean(u, w):
    S = u.shape[1]
    cs = jnp.cumsum(u.astype(jnp.float32), axis=1)
    lagged = jnp.pad(cs, ((0, 0), (w, 0), (0, 0)))[:, :S]
    count = jnp.minimum(jnp.arange(1, S + 1), w).astype(jnp.float32)
    return ((cs - lagged) / count[None, :, None]).astype(u.dtype)


def pool_mixer(u, pool_w, pool_scale):
    B, S, _ = u.shape
    groups = u.reshape(B, S, POOL_GROUPS, POOL_GROUP_DIM)
    pooled = jnp.stack([causal_window_mean(groups[:, :, g], POOL_WINDOWS[g])
                        for g in range(POOL_GROUPS)], axis=2)
    mixed = jnp.einsum('bsgc,gcd->bsgd', pooled - groups, pool_w)
    return mixed.reshape(B, S, POOL_WIDTH) * pool_scale


def forgetting_attention(q, k, v, log_f):
    B, S, H, Dh = q.shape
    c = jnp.cumsum(log_f, axis=1).transpose(0, 2, 1)
    scale = Dh ** -0.5
    outs = []
    for i in range(S // Q_BLOCK):
        q0, q1 = i * Q_BLOCK, (i + 1) * Q_BLOCK
        logits = jnp.einsum('bqhd,bkhd->bhqk', q[:, q0:q1], k[:, :q1]).astype(jnp.float32) * scale
        logits = logits + c[:, :, q0:q1, None] - c[:, :, None, :q1]
        causal = (q0 + jnp.arange(Q_BLOCK))[:, None] >= jnp.arange(q1)[None, :]
        logits = jnp.where(causal[None, None], logits, -jnp.inf)
        p = jax.nn.softmax(logits, axis=-1).astype(v.dtype)
        outs.append(jnp.einsum('bhqk,bkhd->bqhd', p, v[:, :q1]))
    return jnp.concatenate(outs, axis=1)


def memory_attention(q, k, v):
    scale = q.shape[-1] ** -0.5
    logits = jnp.einsum('bshd,bmhd->bhsm', q, k).astype(jnp.float32) * scale
    p = jax.nn.softmax(logits, axis=-1).astype(v.dtype)
    return jnp.einsum('bhsm,bmhd->bshd', p, v)


def setup_inputs(seed: int = 0) -> dict:
    key = jax.random.key(seed)
    ks = jax.random.split(key, 24)
    nrm = lambda k, shape, fan_in: jax.random.normal(k, shape, jnp.float32) * fan_in ** -0.5
    gain = lambda k, shape: 1.0 + 0.1 * jax.random.normal(k, shape, jnp.float32)
    L = DEPTH
    return {
        "x": jax.random.normal(ks[0], (BATCH, SEQ, D_MODEL), jnp.float32),
        "mem": jax.random.normal(ks[1], (BATCH, MEM_LEN, D_MODEL), jnp.float32),
        "ffn1_norm": gain(ks[2], (L, D_MODEL)),
        "ffn1_w_gate_up": nrm(ks[3], (L, D_MODEL, 2 * D_FF), D_MODEL),
        "ffn1_w_down": nrm(ks[4], (L, D_FF, D_MODEL), D_FF),
        "mix_norm": gain(ks[5], (L, D_MODEL)),
        "mem_norm": gain(ks[6], (L, D_MODEL)),
        "w_in": nrm(ks[7], (L, D_MODEL, IN_WIDTH), D_MODEL),
        "b_forget": 2.0 + 0.1 * jax.random.normal(ks[8], (L, FOX_HEADS), jnp.float32),
        "pool_w": nrm(ks[9], (L, POOL_GROUPS, POOL_GROUP_DIM, POOL_GROUP_DIM), POOL_GROUP_DIM),
        "pool_scale": gain(ks[10], (L, POOL_WIDTH)),
        "w_pool_up": nrm(ks[11], (L, POOL_WIDTH, D_MODEL), POOL_WIDTH),
        "fox_q_norm": gain(ks[12], (L, FOX_HEAD_DIM)),
        "fox_k_norm": gain(ks[13], (L, FOX_HEAD_DIM)),
        "w_fox_o": nrm(ks[14], (L, FOX_WIDTH, D_MODEL), FOX_WIDTH),
        "w_mem_kv": nrm(ks[15], (L, D_MODEL, 2 * MEM_WIDTH), D_MODEL),
        "mem_q_norm": gain(ks[16], (L, MEM_HEAD_DIM)),
        "mem_k_norm": gain(ks[17], (L, MEM_HEAD_DIM)),
        "w_mem_o": nrm(ks[18], (L, MEM_WIDTH, D_MODEL), MEM_WIDTH),
        "w_out": nrm(ks[19], (L, D_MODEL, D_MODEL), D_MODEL),
        "ffn2_norm": gain(ks[20], (L, D_MODEL)),
        "ffn2_w_gate_up": nrm(ks[21], (L, D_MODEL, 2 * D_FF), D_MODEL),
        "ffn2_w_down": nrm(ks[22], (L, D_FF, D_MODEL), D_FF),
    }


def reference(x, mem, ffn1_norm, ffn1_w_gate_up, ffn1_w_down, mix_norm, mem_norm, w_in,
              b_forget, pool_w, pool_scale, w_pool_up, fox_q_norm, fox_k_norm, w_fox_o,
              w_mem_kv, mem_q_norm, mem_k_norm, w_mem_o, w_out,
              ffn2_norm, ffn2_w_gate_up, ffn2_w_down):
    B, S, _ = x.shape
    M = mem.shape[1]
    split_idx = list(np.cumsum(IN_SPLITS)[:-1])
    for l in range(DEPTH):
        x = x + swiglu_half_ffn(x, ffn1_norm[l], ffn1_w_gate_up[l], ffn1_w_down[l])

        h = rmsnorm(x, mix_norm[l])
        u_pool, q_f, k_f, v_f, f_logit, q_m, gate_logit = jnp.split(h @ w_in[l], split_idx, axis=-1)

        y_pool = pool_mixer(u_pool, pool_w[l], pool_scale[l]) @ w_pool_up[l]

        q_f = rmsnorm(q_f.reshape(B, S, FOX_HEADS, FOX_HEAD_DIM), fox_q_norm[l])
        k_f = rmsnorm(k_f.reshape(B, S, FOX_HEADS, FOX_HEAD_DIM), fox_k_norm[l])
        v_f = v_f.reshape(B, S, FOX_HEADS, FOX_HEAD_DIM)
        log_f = jax.nn.log_sigmoid(f_logit.astype(jnp.float32) + b_forget[l].astype(jnp.float32))
        y_fox = forgetting_attention(q_f, k_f, v_f, log_f).reshape(B, S, FOX_WIDTH) @ w_fox_o[l]

        k_m, v_m = jnp.split(rmsnorm(mem, mem_norm[l]) @ w_mem_kv[l], 2, axis=-1)
        q_m = rmsnorm(q_m.reshape(B, S, MEM_HEADS, MEM_HEAD_DIM), mem_q_norm[l])
        k_m = rmsnorm(k_m.reshape(B, M, MEM_HEADS, MEM_HEAD_DIM), mem_k_norm[l])
        v_m = v_m.reshape(B, M, MEM_HEADS, MEM_HEAD_DIM)
        y_mem = memory_attention(q_m, k_m, v_m).reshape(B, S, MEM_WIDTH) @ w_mem_o[l]

        g_pool, g_fox, g_mem = jnp.split(jax.nn.sigmoid(gate_logit), N_BRANCHES, axis=-1)
        merged = g_pool * y_pool + g_fox * y_fox + g_mem * y_mem
        x = x + merged @ w_out[l]

        x = x + swiglu_half_ffn(x, ffn2_norm[l], ffn2_w_gate_up[l], ffn2_w_down[l])
    return x
```

```python
from contextlib import ExitStack
import numpy as np
import concourse.bass as bass
import concourse.mybir as mybir
from concourse.bass_utils import run_bass_kernel_spmd

F32 = mybir.dt.float32
BF16 = mybir.dt.bfloat16
ALU = mybir.AluOpType
AF = mybir.ActivationFunctionType
ENGS = ("pe", "act", "dve", "pool", "sp")
EPS = 1e-6
POOL_WINDOWS = (2, 4, 8, 16)


class Cfg:
    def __init__(self, D=2048, FF=5632, S=2048, NSEQ=2, HF=16, HM=4, M=256, T=512, TM=256, G=22, NS=4):
        self.D, self.FF, self.S, self.NSEQ, self.HF, self.HM, self.M = D, FF, S, NSEQ, HF, HM, M
        self.T, self.TM, self.G, self.NS = T, TM, G, NS
        self.KC = D // 128
        self.FC = FF // 128
        self.FW = HF * 64
        self.MW = HM * 128
        self.NPT = 2
        self.NQT = self.FW // 256
        self.NMT = self.MW // 256
        self.NGRP = self.FC // G
        self.n_ffn = self.FC + self.NGRP * self.KC
        self.n_mix = self.NPT + 3 * self.NQT + self.NMT + 2 * self.KC + self.KC // 2
        self.ffn_base = {1: 0, 2: self.n_ffn + self.n_mix}
        self.mix_base = self.n_ffn
        self.mk_base = 2 * self.n_ffn + self.n_mix
        self.NTILES = self.mk_base + 2 * self.NMT
        gb = self.KC * 128 + 4 * 128 + (HF // 2) * 128 + HM * 128
        self.LINE = max(2 * self.KC * 128, G * 128, self.KC * 256, gb)
        self.LINE += self.LINE % 2
        self.o_pool = 0
        self.o_q = 512
        self.o_k = 512 + self.FW
        self.o_v = 512 + 2 * self.FW
        self.o_f = 512 + 3 * self.FW
        self.o_qm = self.o_f + HF
        self.o_g = self.o_qm + self.MW
        c = 0
        self.p_g1 = c; c += self.KC
        self.p_gmix = c; c += self.KC
        self.p_g2 = c; c += self.KC
        self.p_gmem = c; c += self.KC
        self.p_psc = c; c += 4
        self.p_fq = c; c += 1
        self.p_fk = c; c += 1
        self.p_mq = c; c += 1
        self.p_mk = c; c += 1
        self.p_bf = c; c += HF
        self.p_rc = c; c += 64
        self.NPRM = c
        self.NSW = self.KC * HF + 4 * 128


def pack_weights(inp, cfg):
    KC, FC, G, D, FF = cfg.KC, cfg.FC, cfg.G, cfg.D, cfg.FF
    LINE = cfg.LINE
    wts = np.zeros((cfg.NTILES, 128, LINE), np.float32)

    def ffn_tiles(wgu, wd, base):
        gu = wgu.reshape(KC, 128, 2, FC, 128).transpose(3, 1, 2, 0, 4).reshape(FC, 128, 2 * KC * 128)
        dn = wd.reshape(cfg.NGRP, G, 128, KC, 128).transpose(0, 3, 2, 1, 4).reshape(cfg.NGRP, KC, 128, G * 128)
        j = base
        for grp in range(cfg.NGRP):
            wts[j:j + G, :, :2 * KC * 128] = gu[grp * G:(grp + 1) * G]
            j += G
            wts[j:j + KC, :, :G * 128] = dn[grp]
            j += KC

    ffn_tiles(inp["ffn1_w_gate_up"][0], inp["ffn1_w_down"][0], cfg.ffn_base[1])
    ffn_tiles(inp["ffn2_w_gate_up"][0], inp["ffn2_w_down"][0], cfg.ffn_base[2])
    win = inp["w_in"][0]

    def in_tile(c0):
        return win[:, c0:c0 + 256].reshape(KC, 128, 256).transpose(1, 0, 2).reshape(128, KC * 256)

    j = cfg.mix_base
    for off, n in ((cfg.o_pool, cfg.NPT), (cfg.o_q, cfg.NQT), (cfg.o_k, cfg.NQT), (cfg.o_v, cfg.NQT), (cfg.o_qm, cfg.NMT)):
        for t in range(n):
            wts[j, :, :KC * 256] = in_tile(off + 256 * t)
            j += 1
    wpu, wfo, wmo = inp["w_pool_up"][0], inp["w_fox_o"][0], inp["w_mem_o"][0]
    for dc in range(KC):
        ga = np.stack([win[:, cfg.o_g + b * D + dc * 128: cfg.o_g + b * D + (dc + 1) * 128] for b in (0, 1)])
        wts[j, :, :2 * KC * 128] = ga.reshape(2, KC, 128, 128).transpose(2, 0, 1, 3).reshape(128, -1)
        j += 1
        gm = win[:, cfg.o_g + 2 * D + dc * 128: cfg.o_g + 2 * D + (dc + 1) * 128]
        parts = [gm.reshape(KC, 128, 128).transpose(1, 0, 2).reshape(128, -1)]
        for w in (wpu, wfo, wmo):
            kk = w.shape[0] // 128
            parts.append(w[:, dc * 128:(dc + 1) * 128].reshape(kk, 128, 128).transpose(1, 0, 2).reshape(128, -1))
        gb = np.concatenate(parts, axis=1)
        wts[j, :, :gb.shape[1]] = gb
        j += 1
    wout = inp["w_out"][0]
    for t in range(KC // 2):
        wts[j, :, :KC * 256] = wout[:, t * 256:(t + 1) * 256].reshape(KC, 128, 256).transpose(1, 0, 2).reshape(128, -1)
        j += 1
    assert j == cfg.ffn_base[2]
    wkv = inp["w_mem_kv"][0]
    j = cfg.mk_base
    for t in range(2 * cfg.NMT):
        wts[j, :, :KC * 256] = wkv[:, t * 256:(t + 1) * 256].reshape(KC, 128, 256).transpose(1, 0, 2).reshape(128, -1)
        j += 1
    sw = np.zeros((128, cfg.NSW), np.float32)
    sw[:, :KC * cfg.HF] = win[:, cfg.o_f:cfg.o_f + cfg.HF].reshape(KC, 128, cfg.HF).transpose(1, 0, 2).reshape(128, -1)
    sw[:, KC * cfg.HF:] = inp["pool_w"][0].transpose(1, 0, 2).reshape(128, 4 * 128)
    prm = np.zeros((128, cfg.NPRM), np.float32)
    col = lambda v: np.asarray(v, np.float32).reshape(-1, 128).T
    prm[:, cfg.p_g1:cfg.p_g1 + KC] = col(inp["ffn1_norm"][0])
    prm[:, cfg.p_gmix:cfg.p_gmix + KC] = col(inp["mix_norm"][0])
    prm[:, cfg.p_g2:cfg.p_g2 + KC] = col(inp["ffn2_norm"][0])
    prm[:, cfg.p_gmem:cfg.p_gmem + KC] = col(inp["mem_norm"][0])
    prm[:, cfg.p_psc:cfg.p_psc + 4] = col(inp["pool_scale"][0])
    prm[:, cfg.p_fq] = np.tile(np.asarray(inp["fox_q_norm"][0], np.float32), 2)
    prm[:, cfg.p_fk] = np.tile(np.asarray(inp["fox_k_norm"][0], np.float32), 2)
    prm[:, cfg.p_mq] = np.asarray(inp["mem_q_norm"][0], np.float32)
    prm[:, cfg.p_mk] = np.asarray(inp["mem_k_norm"][0], np.float32)
    prm[:, cfg.p_bf:cfg.p_bf + cfg.HF] = np.asarray(inp["b_forget"][0], np.float32)[None, :]
    for g, w in enumerate(POOL_WINDOWS):
        prm[:, cfg.p_rc + 16 * g: cfg.p_rc + 16 * g + 16] = (1.0 / np.minimum(np.arange(1, 17), w))[None, :]
    return wts, sw, prm


class Prog:
    def __init__(self, nc, stack):
        self.nc, self.stack = nc, stack
        self.ops = {e: [] for e in ENGS}
        self.count = {e: 0 for e in ENGS}
        self.seen = {e: {} for e in ENGS}
        self.sem = {}
        self.dcount = {}
        self.buf = {}
        self.alias = {}

    def _expand(self, keys):
        out = []
        for k in keys:
            out.append(k)
            out.extend(self.alias.get(k, ()))
        return out

    def _sem(self, src):
        if src not in self.sem:
            self.sem[src] = self.stack.enter_context(self.nc.semaphore("s_" + src.replace(":", "_")))
        return self.sem[src]

    def op(self, eng, fn, reads=(), writes=(), signal=True, dma=None):
        reads, writes = self._expand(reads), self._expand(writes)
        deps = {}

        def add(tok):
            if tok is not None and deps.get(tok[0], 0) < tok[1]:
                deps[tok[0]] = tok[1]

        for k in reads:
            st = self.buf.get(k)
            if st is not None:
                add(st[0])
        for k in writes:
            st = self.buf.get(k)
            if st is not None:
                add(st[0])
                for t in st[1]:
                    add(t)
        waits = []
        seen = self.seen[eng]
        for s, v in deps.items():
            if seen.get(s, 0) >= v or (s == "pe" and eng == "pe"):
                continue
            seen[s] = v
            waits.append((s, v))
        if dma is not None:
            src = "dma:" + dma
            self.dcount[src] = self.dcount.get(src, 0) + 16
            tok, inc = (src, self.dcount[src]), (src, 16)
        elif signal:
            self.count[eng] += 1
            tok, inc = (eng, self.count[eng]), (eng, 1)
        else:
            tok, inc = (eng, self.count[eng] + 1), None
        self.ops[eng].append((waits, fn, inc))
        for k in reads:
            self.buf.setdefault(k, [None, []])[1].append(tok)
        for k in writes:
            self.buf[k] = [tok, []]
        return tok

    def retoken(self, keys_r, keys_w, old_toks, tok):
        olds = set(old_toks)
        for k in keys_r:
            st = self.buf[k]
            st[1] = [t for t in st[1] if t not in olds] + [tok]
        for k in keys_w:
            self.buf[k] = [tok, []]

    def barrier(self, extra=()):
        for e in ("pe", "act", "dve", "pool"):
            waits = []
            for s in ("pe", "act", "dve", "pool"):
                v = self.count[s]
                if s != e and v > self.seen[e].get(s, 0):
                    self.seen[e][s] = v
                    waits.append((s, v))
            for s, v in extra:
                if v > self.seen[e].get(s, 0):
                    self.seen[e][s] = v
                    waits.append((s, v))
            self.ops[e].append((waits, None, None))

    def wait_all(self, eng, toks):
        self.ops[eng].append((list(toks), None, None))

    def emit(self):
        nc, ops, sem = self.nc, self.ops, self._sem
        for s_ in list(self.dcount) + ["pe", "act", "dve", "pool"]:
            sem(s_)

        def run(engine, lst):
            for waits, fn, inc in lst:
                for s, v in waits:
                    engine.wait_ge(sem(s), v)
                if fn is None:
                    continue
                ins = fn(engine)
                if inc is not None:
                    ins.then_inc(sem(inc[0]), inc[1])

        with nc.Block() as block:
            @block.tensor
            def _(e):
                run(e, ops["pe"])

            @block.scalar
            def _(e):
                run(e, ops["act"])

            @block.vector
            def _(e):
                run(e, ops["dve"])

            @block.gpsimd
            def _(e):
                run(e, ops["pool"])

            @block.sync
            def _(e):
                run(e, ops["sp"])


def build_program(cfg, prepass=False):
    c = cfg
    KC, FC, G, T, TM, HF, HM, M, S, NS, LINE = c.KC, c.FC, c.G, c.T, c.TM, c.HF, c.HM, c.M, c.S, c.NS, c.LINE
    NJ = S // 128
    HP = HF // 2
    MC = M // 128
    H2 = LINE // 2
    nc = bass.Bass("TRN2", target_bir_lowering=False)
    xd = nc.dram_tensor("xT", [c.NSEQ, KC, 128, S], F32, kind="ExternalInput").ap()
    md = nc.dram_tensor("memT", [c.NSEQ, KC, 128, M], F32, kind="ExternalInput").ap()
    wts = nc.dram_tensor("wts", [c.NTILES, 128, LINE], F32, kind="ExternalInput").ap()
    swd = nc.dram_tensor("sw", [128, c.NSW], F32, kind="ExternalInput").ap()
    prd = nc.dram_tensor("prm", [128, c.NPRM], F32, kind="ExternalInput").ap()
    od = nc.dram_tensor("outT", [c.NSEQ, KC, 128, S], F32, kind="ExternalOutput").ap()
    wtb = nc.dram_tensor("wtb", [c.NTILES, 128, LINE], BF16).ap()

    with ExitStack() as st:
        P = Prog(nc, st)
        sb = lambda name, shape, dt: st.enter_context(nc.sbuf_tensor(name, shape, dt))
        x = sb("x", [128, KC, T], F32)
        h = sb("h", [128, KC, T], BF16)
        kT = sb("kT", [128, HP, S], BF16)
        Vc = sb("Vc", [128, NJ * c.FW], BF16)
        Vc32 = Vc.bitcast(F32)
        km = sb("km", [128, HM, M], BF16)
        vm = sb("vm", [128, MC, c.MW], BF16)
        ring = [sb("ring%d" % i, [128, LINE], BF16) for i in range(NS)]
        sq = [sb("sq%d" % i, [128, T], BF16) for i in range(2)]
        sd = sb("sd", [128, T], F32)
        rstd = sb("rstd", [128, T], F32)
        NQN = max(TM, M)
        sdq = [sb("sdq%d" % i, [128, NQN], F32) for i in range(2)]
        rstdq = [sb("rstdq%d" % i, [128, NQN], F32) for i in range(2)]
        ones_bf = sb("ones_bf", [128, 128], BF16)
        blk64 = sb("blk64", [128, 128], BF16)
        tri = sb("tri", [128, 128], F32)
        tri_bf = sb("tri_bf", [128, 128], BF16)
        ones_f = sb("ones_f", [128, 128], F32)
        prm = sb("prm_sb", [128, c.NPRM], F32)
        swb = sb("swb", [128, c.NSW], BF16)
        Cneg = sb("Cneg", [128, NJ, HF], F32)
        carry = sb("carry", [128, NJ + 1, HF], F32)
        NQ = TM // 128
        biast = sb("biast", [128, NJ, HF], F32)
        zt = sb("zt", [128, 3, HF], F32)
        uhist = sb("uhist", [128, 4, 16], F32)
        ffn_b = G * T * 2 + 2 * T * 4
        W16 = 16 + TM
        mix_b = (4 * W16 * 4 + 2 * W16 * 4 + 4 * TM * 2 + 4 * TM * 2 + HP * TM * 2 + HM * TM * 2 + HP * TM * 2
                 + HM * TM * 2 + 3 * TM * 2 + 2 * TM * 4 + 3 * TM * 4 + 3 * TM * 4 + KC * TM * 2 + 64 * 4)
        mem_b = KC * M * 4 + KC * M * 2
        pre_b = 3 * H2 * 4 + 3 * H2 * 2
        UB = max(ffn_b, mix_b, mem_b, pre_b)
        UB += (-UB) % 64
        U = sb("U", [128, UB // 2], BF16)
        U32 = U.bitcast(F32)

        class Carve:
            def __init__(self):
                self.off = 0

            def bf(self, n):
                assert self.off % 4 == 0
                a = self.off // 2
                self.off += n * 2
                self.off += (-self.off) % 4
                assert self.off <= UB
                return a, a + n

            def f32(self, n):
                a = self.off // 4
                self.off += n * 4
                assert self.off <= UB
                return a, a + n

        ps = [st.enter_context(nc.psum_tensor("ps%d" % i, [128, 512], F32)) for i in range(8)]
        rr = {"i": 0}

        def bank(pool=(0, 1, 2, 3, 4, 5, 6, 7)):
            rr["i"] += 1
            return pool[rr["i"] % len(pool)]

        pcol = lambda cidx: prm[:, cidx:cidx + 1]

        P.op("sp", lambda e: e.dma_start(out=prm[:, :], in_=prd), writes=["prm"], dma="prm")
        P.op("sp", lambda e: e.dma_start(out=U32[:, 0:c.NSW], in_=swd), writes=[("sf", 0)], dma="swf")
        P.op("dve", lambda e: e.memset(ones_bf[:, :], 1.0), writes=["ones_bf"])
        P.op("dve", lambda e: e.memset(ones_f[:, :], 1.0), writes=["ones_f"])
        P.op("dve", lambda e: e.memset(blk64[:, :], 0.0), writes=["blk64"])
        P.op("dve", lambda e: e.memset(blk64[0:64, 0:64], 1.0), writes=["blk64"])
        P.op("dve", lambda e: e.memset(blk64[64:128, 64:128], 1.0), writes=["blk64"])
        P.op("pool", lambda e: e.affine_select(out=tri[:, :], in_=ones_f[:, :], pattern=[[1, 128]], compare_op=ALU.is_ge,
                                               fill=0.0, base=0, channel_multiplier=-1), reads=["ones_f"], writes=["tri"])
        P.op("dve", lambda e: e.tensor_copy(out=tri_bf[:, :], in_=tri[:, :]), reads=["tri"], writes=["tri_bf"])
        P.op("dve", lambda e: e.tensor_copy(out=swb[:, :], in_=U32[:, 0:c.NSW]), reads=[("sf", 0)], writes=["swb"])
        P.barrier()
        wf_ap = lambda kc: swb[:, kc * HF:(kc + 1) * HF]
        poolw_ap = lambda g: swb[:, KC * HF + g * 128: KC * HF + (g + 1) * 128]

        if prepass:
            cv = Carve()
            sf = [cv.f32(H2) for _ in range(3)]
            sbb = [cv.bf(H2) for _ in range(3)]
            ceng = ("dve", "pool", "act")
            chunks = [(j, hf) for j in range(c.NTILES) for hf in range(2)]
            NCH = len(chunks)

            def pre_load(n):
                j, hf = chunks[n]
                b = n % 3
                fa, fb = sf[b]
                src = wts[j, :, hf * H2:(hf + 1) * H2]
                P.op("sp", lambda e: e.dma_start(out=U32[:, fa:fb], in_=src), writes=[("sf", b)], dma="sf%d" % b)

            def pre_cast_store(n):
                j, hf = chunks[n]
                b = n % 3
                fa, fb = sf[b]
                ba, bb = sbb[b]
                dst = wtb[j, :, hf * H2:(hf + 1) * H2]
                ce = ceng[n % 3]
                if ce == "act":
                    P.op("act", lambda e: e.activation(out=U[:, ba:bb], in_=U32[:, fa:fb], func=AF.Copy), reads=[("sf", b)], writes=[("sbb", b)])
                else:
                    P.op(ce, lambda e: e.tensor_copy(out=U[:, ba:bb], in_=U32[:, fa:fb]), reads=[("sf", b)], writes=[("sbb", b)])
                P.op("sp", lambda e: e.dma_start(out=dst, in_=U[:, ba:bb]), reads=[("sbb", b)], dma="sb%d" % b)

            for n in range(NCH + 2):
                if n < NCH:
                    pre_load(n)
                if n >= 2:
                    pre_cast_store(n - 2)
            extra = [(s_, v) for s_, v in P.dcount.items() if s_.startswith("dma:s")]
            P.wait_all("sp", extra)
            P.barrier(extra)

        wn = {"n": 0}

        fresh_mode = not prepass
        done = set()
        pending = {"s": None}
        stn = {"n": 0}
        PL = LINE // 4
        base32 = (T // 128) * c.FW // 2
        NSTG = 4
        if fresh_mode:
            if (NJ * c.FW // 2 - base32) // PL >= 4:
                NSTG = (NJ * c.FW // 2 - base32) // PL
                stage_ap = lambda b_: Vc32[:, base32 + b_ * PL: base32 + (b_ + 1) * PL]
            else:
                stg_t = sb("stg_t", [128, 4 * PL], F32)
                stage_ap = lambda b_: stg_t[:, b_ * PL:(b_ + 1) * PL]
        cast_eng = ("dve", "act", "pool", "pool")
        for sl_ in range(NS):
            P.alias[("ring", sl_)] = [("ringq", sl_, q_) for q_ in range(4)]

        def flush_store():
            if pending["s"] is not None:
                j_, slot_ = pending["s"]
                pending["s"] = None
                P.op("sp", lambda e: e.dma_start(out=wtb[j_, :, :], in_=ring[slot_][:, :]), reads=[("ring", slot_)], writes=[("wtb", j_)], dma="ws%d" % slot_)

        def wtile(j):
            slot = wn["n"] % NS
            wn["n"] += 1
            if fresh_mode and j not in done:
                done.add(j)
                for q in range(4):
                    b_ = stn["n"] % NSTG
                    stn["n"] += 1
                    sa = stage_ap(b_)
                    src = wts[j, :, q * PL:(q + 1) * PL]
                    P.op("sp", lambda e, sa=sa, src=src: e.dma_start(out=sa, in_=src), writes=[("stg", b_)], dma="stg%d" % b_)
                    ce = cast_eng[q]
                    if ce == "act":
                        P.op("act", lambda e, sa=sa, q=q: e.activation(out=ring[slot][:, q * PL:(q + 1) * PL], in_=sa, func=AF.Copy),
                             reads=[("stg", b_)], writes=[("ringq", slot, q)])
                    else:
                        P.op(ce, lambda e, sa=sa, q=q: e.tensor_copy(out=ring[slot][:, q * PL:(q + 1) * PL], in_=sa),
                             reads=[("stg", b_)], writes=[("ringq", slot, q)])
                flush_store()
                pending["s"] = (j, slot)
            else:
                flush_store()
                P.op("sp", lambda e: e.dma_start(out=ring[slot][:, :], in_=wtb[j, :, :]), reads=[("wtb", j)], writes=[("ring", slot)], dma="w%d" % slot)
            return ring[slot], ("ring", slot)

        def mm_group(out_ap, pairs, reads, bankkey, start=True, stop=True, each=None):
            n = len(pairs)
            for i_, (l, r) in enumerate(pairs):
                rd_ = reads if each is None else list(reads) + [each[i_]]
                P.op("pe", lambda e, l=l, r=r, i_=i_: e.matmul(out_ap, lhsT=l, rhs=r, start=(start and i_ == 0), stop=(stop and i_ == n - 1)),
                     reads=rd_, writes=[bankkey], signal=(i_ == n - 1))

        def norm_finish(ssq_bank, n, inv_n):
            P.op("act", lambda e: e.activation(out=sd[:, :n], in_=ps[ssq_bank][:, :n], func=AF.Sqrt, bias=EPS, scale=inv_n),
                 reads=[("ps", ssq_bank)], writes=["sd"])
            P.op("dve", lambda e: e.reciprocal(out=rstd[:, :n], in_=sd[:, :n]), reads=["sd"], writes=["rstd"])

        def rmsnorm_stream(src_fn, src_keys, dst_fn, dst_keys, gcol0, nk, n, inv_n):
            bs = bank()
            for kc in range(nk):
                b = kc % 2
                P.op("act", lambda e, kc=kc, b=b: e.activation(out=sq[b][:, :n], in_=src_fn(kc), func=AF.Square),
                     reads=[src_keys[kc]], writes=[("sq", b)])
                P.op("pe", lambda e, kc=kc, b=b: e.matmul(ps[bs][:, :n], lhsT=ones_bf[:, :], rhs=sq[b][:, :n], start=(kc == 0), stop=(kc == nk - 1)),
                     reads=[("sq", b), "ones_bf"], writes=[("ps", bs)], signal=True)
            norm_finish(bs, n, inv_n)
            for kc in range(nk):
                P.op("dve", lambda e, kc=kc: e.scalar_tensor_tensor(out=dst_fn(kc), in0=src_fn(kc), scalar=pcol(gcol0 + kc), in1=rstd[:, :n],
                                                                  op0=ALU.mult, op1=ALU.mult),
                     reads=[src_keys[kc], "rstd", "prm"], writes=[dst_keys[kc]])

        def qknorm_a(src_bank, n, k):
            P.op("act", lambda e: e.activation(out=sq[k][:, :n], in_=ps[src_bank][:, :n], func=AF.Square),
                 reads=[("ps", src_bank)], writes=[("sq", k)])

        def qknorm_b(src_bank, n, k, ones_ap, ones_key, inv_n, gcol, dst_ap, dst_key):
            b = bank()
            P.op("pe", lambda e: e.matmul(ps[b][:, :n], lhsT=ones_ap, rhs=sq[k][:, :n], start=True, stop=True),
                 reads=[("sq", k), ones_key], writes=[("ps", b)])
            P.op("act", lambda e: e.activation(out=sdq[k][:, :n], in_=ps[b][:, :n], func=AF.Sqrt, bias=EPS, scale=inv_n),
                 reads=[("ps", b)], writes=[("sdq", k)])
            P.op("dve", lambda e: e.reciprocal(out=rstdq[k][:, :n], in_=sdq[k][:, :n]), reads=[("sdq", k)], writes=[("rstdq", k)])
            P.op("dve", lambda e: e.scalar_tensor_tensor(out=dst_ap, in0=ps[src_bank][:, :n], scalar=pcol(gcol), in1=rstdq[k][:, :n],
                                                         op0=ALU.mult, op1=ALU.mult),
                 reads=[("ps", src_bank), ("rstdq", k), "prm"], writes=[dst_key])

        def qknorm(src_bank, n, ones_ap, ones_key, inv_n, gcol, dst_ap, dst_key):
            qknorm_a(src_bank, n, 0)
            qknorm_b(src_bank, n, 0, ones_ap, ones_key, inv_n, gcol, dst_ap, dst_key)

        def ffn(idx, gcol0):
            cv = Carve()
            act_a, _ = cv.bf(G * T)
            sg = [cv.f32(T) for _ in range(2)]
            rmsnorm_stream(lambda kc: x[:, kc, :], [("x", kc) for kc in range(KC)],
                           lambda kc: h[:, kc, :], [("h", kc) for kc in range(KC)], gcol0, KC, T, 1.0 / c.D)
            hkeys = [("h", kc) for kc in range(KC)]
            j = c.ffn_base[idx]
            for grp in range(c.NGRP):
                for ff in range(G):
                    wt, wk = wtile(j)
                    j += 1
                    bg, bu = bank(), bank()
                    mm_group(ps[bg][:, :T], [(wt[:, kc * 128:(kc + 1) * 128], h[:, kc, :]) for kc in range(KC)], [wk], ("ps", bg), each=hkeys)
                    mm_group(ps[bu][:, :T], [(wt[:, (KC + kc) * 128:(KC + kc + 1) * 128], h[:, kc, :]) for kc in range(KC)], [wk], ("ps", bu), each=hkeys)
                    s0, s1 = sg[ff % 2]
                    P.op("act", lambda e, bg=bg, s0=s0, s1=s1: e.activation(out=U32[:, s0:s1], in_=ps[bg][:, :T], func=AF.Silu),
                         reads=[("ps", bg)], writes=[("sg", ff % 2)])
                    a0 = act_a + ff * T
                    P.op("dve", lambda e, bu=bu, s0=s0, s1=s1, a0=a0: e.tensor_tensor(out=U[:, a0:a0 + T], in0=ps[bu][:, :T], in1=U32[:, s0:s1], op=ALU.mult),
                         reads=[("ps", bu), ("sg", ff % 2)], writes=[("act", ff)])
                akeys = [("act", ff) for ff in range(G)]
                for dc in range(KC):
                    wt, wk = wtile(j)
                    j += 1
                    bd = bank()
                    mm_group(ps[bd][:, :T], [(wt[:, kk * 128:(kk + 1) * 128], U[:, act_a + kk * T: act_a + (kk + 1) * T]) for kk in range(G)],
                             [wk], ("ps", bd), each=akeys)
                    P.op("dve", lambda e, bd=bd, dc=dc: e.scalar_tensor_tensor(out=x[:, dc, :], in0=ps[bd][:, :T], scalar=0.5, in1=x[:, dc, :],
                                                                             op0=ALU.mult, op1=ALU.add),
                         reads=[("ps", bd), ("x", dc)], writes=[("x", dc)])
            P.barrier()

        def memkv(s):
            cv = Carve()
            mf0, _ = cv.f32(KC * M)
            hm0, _ = cv.bf(KC * M)
            toks = []
            step = max(1, KC // 4)
            for k0 in range(0, KC, step):
                toks.append(P.op("act", lambda e, k0=k0: e.dma_start(
                    out=U32[:, mf0 + k0 * M: mf0 + (k0 + step) * M].rearrange("p (k m) -> p k m", k=step),
                    in_=md[s, k0:k0 + step, :, :].rearrange("k p m -> p k m")), writes=["memf"], dma="mem"))
            P.retoken([], ["memf"], toks, toks[-1])
            mfk = ["memf"] * KC
            rmsnorm_stream(lambda kc: U32[:, mf0 + kc * M: mf0 + (kc + 1) * M], mfk,
                           lambda kc: U[:, hm0 + kc * M: hm0 + (kc + 1) * M], [("hm", kc) for kc in range(KC)], c.p_gmem, KC, M, 1.0 / c.D)
            hmk = [("hm", kc) for kc in range(KC)]
            hm_ap = lambda kc, a, b: U[:, hm0 + kc * M + a: hm0 + kc * M + b]
            for t in range(c.NMT):
                wt, wk = wtile(c.mk_base + t)
                for cc in range(2):
                    hd = 2 * t + cc
                    b = bank()
                    mm_group(ps[b][:, :M], [(wt[:, kc * 256 + cc * 128: kc * 256 + (cc + 1) * 128], hm_ap(kc, 0, M)) for kc in range(KC)],
                             hmk + [wk], ("ps", b))
                    qknorm(b, M, ones_bf[:, :], "ones_bf", 1.0 / 128, c.p_mk, km[:, hd, :], ("km", hd))
            for t in range(c.NMT):
                wt, wk = wtile(c.mk_base + c.NMT + t)
                for mc in range(MC):
                    b = bank()
                    mm_group(ps[b][:, :256], [(hm_ap(kc, mc * 128, (mc + 1) * 128), wt[:, kc * 256:(kc + 1) * 256]) for kc in range(KC)],
                             hmk + [wk], ("ps", b))
                    P.op("act", lambda e, b=b, mc=mc, t=t: e.activation(out=vm[:, mc, t * 256:(t + 1) * 256], in_=ps[b][:, :256], func=AF.Copy),
                         reads=[("ps", b)], writes=[("vm", mc)])
            P.barrier()

        def mixer(s, i, c0):
            t0 = i * T + c0
            jq0 = t0 // 128
            cv = Carve()
            ub0, _ = cv.f32(4 * W16)
            lv = [cv.f32(W16)[0] for _ in range(2)]
            t16, _ = cv.f32(64)
            db0, _ = cv.bf(4 * TM)
            mx0, _ = cv.bf(4 * TM)
            q0, _ = cv.bf(HP * TM)
            qm0, _ = cv.bf(HM * TM)
            at0, _ = cv.bf(HP * TM)
            mo0, _ = cv.bf(HM * TM)
            pt = [cv.bf(TM)[0] for _ in range(3)]
            rd = [cv.f32(TM)[0] for _ in range(2)]
            sgt = [cv.f32(TM)[0] for _ in range(3)]
            tt = [cv.f32(TM)[0] for _ in range(3)]
            mg0, _ = cv.bf(KC * TM)
            hc = lambda kc: h[:, kc, c0:c0 + TM]
            hkeys = [("h", kc) for kc in range(KC)]
            ub = lambda g, a, b: U32[:, ub0 + g * W16 + a: ub0 + g * W16 + b]
            j = c.mix_base

            for g in range(4):
                if t0 == 0:
                    P.op("pool", lambda e, g=g: e.memset(ub(g, 0, 16), 0.0), writes=[("ub", g)])
                else:
                    P.op("pool", lambda e, g=g: e.tensor_copy(out=ub(g, 0, 16), in_=uhist[:, g, :]), reads=[("uhist", g)], writes=[("ub", g)])
            for t in range(c.NPT):
                wt, wk = wtile(j)
                j += 1
                for cc in range(2):
                    g = 2 * t + cc
                    b = bank()
                    mm_group(ps[b][:, :TM], [(wt[:, kc * 256 + cc * 128: kc * 256 + (cc + 1) * 128], hc(kc)) for kc in range(KC)], hkeys + [wk], ("ps", b))
                    P.op("act", lambda e, b=b, g=g: e.activation(out=ub(g, 16, W16), in_=ps[b][:, :TM], func=AF.Copy),
                         reads=[("ps", b)], writes=[("ub", g)])
            for g in range(4):
                w = POOL_WINDOWS[g]
                cur = lambda a, b, g=g: ub(g, a, b)
                curk = ("ub", g)
                for lvl in range(g + 1):
                    sh = 1 << lvl
                    dst0 = lv[lvl % 2]
                    dst = lambda a, b, dst0=dst0: U32[:, dst0 + a: dst0 + b]
                    P.op("pool", lambda e, cur=cur, dst=dst, sh=sh: e.tensor_tensor(out=dst(sh, W16), in0=cur(sh, W16), in1=cur(0, W16 - sh), op=ALU.add),
                         reads=[curk], writes=[("lv", lvl % 2)])
                    cur, curk = dst, ("lv", lvl % 2)
                P.op("dve", lambda e, cur=cur, g=g, w=w: e.scalar_tensor_tensor(out=U[:, db0 + g * TM: db0 + (g + 1) * TM], in0=cur(16, W16), scalar=1.0 / w,
                                                                              in1=ub(g, 16, W16), op0=ALU.mult, op1=ALU.subtract),
                     reads=[curk, ("ub", g)], writes=[("db", g)])
                if t0 == 0:
                    P.op("dve", lambda e, cur=cur, g=g: e.tensor_tensor(out=U32[:, t16 + 16 * g: t16 + 16 * g + 16], in0=cur(16, 32),
                                                                      in1=prm[:, c.p_rc + 16 * g: c.p_rc + 16 * g + 16], op=ALU.mult),
                         reads=[curk, "prm"], writes=[("t16", g)])
                    P.op("dve", lambda e, g=g: e.tensor_tensor(out=U[:, db0 + g * TM: db0 + g * TM + 16], in0=U32[:, t16 + 16 * g: t16 + 16 * g + 16],
                                                             in1=ub(g, 16, 32), op=ALU.subtract),
                         reads=[("t16", g), ("ub", g)], writes=[("db", g)])
                b = bank()
                P.op("pe", lambda e, b=b, g=g: e.matmul(ps[b][:, :TM], lhsT=poolw_ap(g), rhs=U[:, db0 + g * TM: db0 + (g + 1) * TM], start=True, stop=True),
                     reads=[("db", g), "swb"], writes=[("ps", b)])
                P.op("dve", lambda e, b=b, g=g: e.tensor_scalar(out=U[:, mx0 + g * TM: mx0 + (g + 1) * TM], in0=ps[b][:, :TM], scalar1=pcol(c.p_psc + g),
                                                              scalar2=None, op0=ALU.mult),
                     reads=[("ps", b), "prm"], writes=[("mx", g)])
                P.op("pool", lambda e, g=g: e.tensor_copy(out=uhist[:, g, :], in_=ub(g, TM, TM + 16)), reads=[("ub", g)], writes=[("uhist", g)])

            jb_q = c.mix_base + c.NPT
            jobs = []
            for t in range(c.NQT):
                for cc in range(2):
                    hp = 2 * t + cc
                    jobs.append((jb_q + t, cc, blk64[:, :], "blk64", 1.0 / 64, c.p_fq, U[:, q0 + hp * TM: q0 + (hp + 1) * TM], ("q", hp)))
            for t in range(c.NQT):
                for cc in range(2):
                    hp = 2 * t + cc
                    jobs.append((jb_q + c.NQT + t, cc, blk64[:, :], "blk64", 1.0 / 64, c.p_fk, kT[:, hp, t0:t0 + TM], ("kT", hp)))
            for t in range(c.NMT):
                for cc in range(2):
                    hd = 2 * t + cc
                    jobs.append((jb_q + 3 * c.NQT + t, cc, ones_bf[:, :], "ones_bf", 1.0 / 128, c.p_mq, U[:, qm0 + hd * TM: qm0 + (hd + 1) * TM], ("qm", hd)))
            pend = None
            cur_w = None
            for jn, (jt, cc, ones_ap, ones_key, inv_n, gcol, dst_ap, dst_key) in enumerate(jobs):
                if cc == 0:
                    cur_w = wtile(jt)
                wt, wk = cur_w
                b = bank()
                mm_group(ps[b][:, :TM], [(wt[:, kc * 256 + cc * 128: kc * 256 + (cc + 1) * 128], hc(kc)) for kc in range(KC)], hkeys + [wk], ("ps", b))
                qknorm_a(b, TM, jn % 2)
                if pend is not None:
                    qknorm_b(*pend)
                pend = (b, TM, jn % 2, ones_ap, ones_key, inv_n, gcol, dst_ap, dst_key)
            qknorm_b(*pend)
            j = jb_q + 2 * c.NQT
            for t in range(c.NQT):
                wt, wk = wtile(j)
                j += 1
                for tc in range(NQ):
                    b = bank()
                    mm_group(ps[b][:, :256], [(h[:, kc, c0 + tc * 128: c0 + (tc + 1) * 128], wt[:, kc * 256:(kc + 1) * 256]) for kc in range(KC)],
                             hkeys + [wk], ("ps", b))
                    P.op("act", lambda e, b=b, tc=tc, t=t: e.activation(out=Vc[:, (jq0 + tc) * c.FW + t * 256:(jq0 + tc) * c.FW + (t + 1) * 256], in_=ps[b][:, :256], func=AF.Copy),
                         reads=[("ps", b)], writes=[("Vc", jq0 + tc)])
            if t0 == 0:
                P.op("dve", lambda e: e.memset(carry[:, 0, :], 0.0), writes=[("carry", 0)])
            for tc in range(NQ):
                jj = jq0 + tc
                b = bank()
                mm_group(ps[b][:, :HF], [(h[:, kc, c0 + tc * 128: c0 + (tc + 1) * 128], wf_ap(kc)) for kc in range(KC)], hkeys + ["swb"], ("ps", b))
                P.op("dve", lambda e, b=b: e.tensor_tensor(out=zt[:, 0, :], in0=ps[b][:, :HF], in1=prm[:, c.p_bf:c.p_bf + HF], op=ALU.add),
                     reads=[("ps", b), "prm"], writes=["z0"])
                P.op("act", lambda e: e.activation(out=zt[:, 1, :], in_=zt[:, 0, :], func=AF.Exp, scale=-1.0), reads=["z0"], writes=["z1"])
                P.op("act", lambda e: e.activation(out=zt[:, 2, :], in_=zt[:, 1, :], func=AF.Ln, bias=1.0, scale=1.0), reads=["z1"], writes=["z2"])
                b1, b2 = bank(), bank()
                P.op("pe", lambda e, b1=b1: e.matmul(ps[b1][:, :HF], lhsT=tri[:, :], rhs=zt[:, 2, :], start=True, stop=True),
                     reads=["z2", "tri"], writes=[("ps", b1)])
                P.op("pe", lambda e, b2=b2: e.matmul(ps[b2][:, :HF], lhsT=ones_f[:, :], rhs=zt[:, 2, :], start=True, stop=True),
                     reads=["z2", "ones_f"], writes=[("ps", b2)])
                P.op("dve", lambda e, b1=b1, jj=jj: e.tensor_tensor(out=Cneg[:, jj, :], in0=ps[b1][:, :HF], in1=carry[:, jj, :], op=ALU.add),
                     reads=[("ps", b1), ("carry", jj)], writes=[("Cneg", jj)])
                P.op("dve", lambda e, b2=b2, jj=jj: e.tensor_tensor(out=carry[:, jj + 1, :], in0=ps[b2][:, :HF], in1=carry[:, jj, :], op=ALU.add),
                     reads=[("ps", b2), ("carry", jj)], writes=[("carry", jj + 1)])
            for jk in range(jq0 + NQ):
                P.op("dve", lambda e, jk=jk: e.tensor_tensor(out=biast[:, jk, :], in0=Cneg[:, jk, :], in1=carry[:, jq0 + 1, :], op=ALU.subtract),
                     reads=[("Cneg", jk), ("carry", jq0 + 1)], writes=[("bias", jk)])
            j = c.mix_base + c.NPT + 3 * c.NQT + c.NMT

            nk = jq0 + NQ
            assert NQ <= 2
            spool = (0, 1, 2, 3)
            its = [("f", hp, hh, jk) for hp in range(HP) for hh in range(2) for jk in range(nk)]
            its += [("m", hd, 0, mc) for hd in range(HM) for mc in range(MC)]

            def it_qk(n):
                kind, a_, hh, jk = its[n]
                bs = spool[n % 4]
                if kind == "f":
                    hp, r0 = a_, 64 * hh
                    cq = max(0, jk - jq0) * 128
                    P.op("pe", lambda e: e.matmul(ps[bs][:, cq:TM], lhsT=kT[r0:r0 + 64, hp, jk * 128:(jk + 1) * 128],
                                                  rhs=U[r0:r0 + 64, q0 + hp * TM + cq: q0 + (hp + 1) * TM], start=True, stop=True),
                         reads=[("kT", hp), ("q", hp)], writes=[("ps", bs)])
                else:
                    hd, mc = a_, jk
                    P.op("pe", lambda e: e.matmul(ps[bs][:, :TM], lhsT=km[:, hd, mc * 128:(mc + 1) * 128],
                                                  rhs=U[:, qm0 + hd * TM: qm0 + (hd + 1) * TM], start=True, stop=True),
                         reads=[("km", hd), ("qm", hd)], writes=[("ps", bs)])

            def it_rest(n):
                kind, a_, hh, jk = its[n]
                bs = spool[n % 4]
                pb = n % 3
                p0 = pt[pb]
                if kind == "f":
                    hp, r0 = a_, 64 * hh
                    hd = 2 * hp + hh
                    gi = hp
                    cq = max(0, jk - jq0) * 128
                    first, lastk = (jk == 0), (jk == nk - 1)
                    P.op("act", lambda e: e.activation(out=U[:, p0 + cq: p0 + TM], in_=ps[bs][:, cq:TM], func=AF.Exp,
                                                       bias=biast[:, jk, hd:hd + 1], scale=0.125),
                         reads=[("ps", bs), ("bias", jk)], writes=[("pt", pb)])
                    if jk >= jq0:
                        jb = jk - jq0
                        P.op("pool", lambda e: e.tensor_tensor(out=U[:, p0 + jb * 128: p0 + (jb + 1) * 128], in0=U[:, p0 + jb * 128: p0 + (jb + 1) * 128],
                                                               in1=tri_bf[:, :], op=ALU.mult),
                             reads=[("pt", pb), "tri_bf"], writes=[("pt", pb)])
                    vl, vkey, ol, rows = Vc[:, jk * c.FW + hd * 64: jk * c.FW + (hd + 1) * 64], ("Vc", jk), ones_bf[:, 0:64], slice(r0, r0 + 64)
                    fin = lastk and hh == 1
                    dst0, dkey = at0 + hp * TM, ("at", hp)
                else:
                    hd, mc = a_, jk
                    gi = HP + hd
                    cq = 0
                    first, lastk = (mc == 0), (mc == MC - 1)
                    P.op("act", lambda e: e.activation(out=U[:, p0: p0 + TM], in_=ps[bs][:, :TM], func=AF.Exp, scale=128.0 ** -0.5),
                         reads=[("ps", bs)], writes=[("pt", pb)])
                    vl, vkey, ol, rows = vm[:, mc, hd * 128:(hd + 1) * 128], ("vm", mc), ones_bf[:, :], slice(0, 128)
                    fin = lastk
                    dst0, dkey = mo0 + hd * TM, ("mo", hd)
                bn, bd = (4, 5) if gi % 2 == 0 else (6, 7)
                P.op("pe", lambda e: e.matmul(ps[bn][rows, cq:TM], lhsT=vl, rhs=U[:, p0 + cq: p0 + TM], start=first, stop=lastk),
                     reads=[("pt", pb), vkey], writes=[("ps", bn)], signal=False)
                P.op("pe", lambda e: e.matmul(ps[bd][rows, cq:TM], lhsT=ol, rhs=U[:, p0 + cq: p0 + TM], start=first, stop=lastk),
                     reads=[("pt", pb), "ones_bf"], writes=[("ps", bd)], signal=True)
                if fin:
                    r_ = rd[gi % 2]
                    P.op("dve", lambda e: e.reciprocal(out=U32[:, r_:r_ + TM], in_=ps[bd][:, :TM]), reads=[("ps", bd)], writes=[("rd", gi % 2)])
                    P.op("dve", lambda e: e.tensor_tensor(out=U[:, dst0: dst0 + TM], in0=ps[bn][:, :TM], in1=U32[:, r_:r_ + TM], op=ALU.mult),
                         reads=[("ps", bn), ("rd", gi % 2)], writes=[dkey])

            LA = 2
            for n in range(len(its) + LA):
                if n < len(its):
                    it_qk(n)
                if n >= LA:
                    it_rest(n - LA)

            for dc in range(KC):
                wa, wak = wtile(j)
                wb, wbk = wtile(j + 1)
                j += 2
                o = KC * 128
                branches = (
                    ([wa[:, kc * 128:(kc + 1) * 128] for kc in range(KC)], wak,
                     [(wb[:, o + kk * 128: o + (kk + 1) * 128], U[:, mx0 + kk * TM: mx0 + (kk + 1) * TM]) for kk in range(4)], [("mx", g) for g in range(4)]),
                    ([wa[:, (KC + kc) * 128:(KC + kc + 1) * 128] for kc in range(KC)], wak,
                     [(wb[:, o + (4 + kk) * 128: o + (5 + kk) * 128], U[:, at0 + kk * TM: at0 + (kk + 1) * TM]) for kk in range(HP)], [("at", g) for g in range(HP)]),
                    ([wb[:, kc * 128:(kc + 1) * 128] for kc in range(KC)], wbk,
                     [(wb[:, o + (4 + HP + kk) * 128: o + (5 + HP + kk) * 128], U[:, mo0 + kk * TM: mo0 + (kk + 1) * TM]) for kk in range(HM)], [("mo", g) for g in range(HM)]),
                )
                for bi, (gws, gk, ypairs, ykeys) in enumerate(branches):
                    bg, by = bank(), bank()
                    mm_group(ps[bg][:, :TM], [(gws[kc], hc(kc)) for kc in range(KC)], hkeys + [gk], ("ps", bg))
                    mm_group(ps[by][:, :TM], ypairs, ykeys + [wbk], ("ps", by))
                    P.op("act", lambda e, bg=bg, bi=bi: e.activation(out=U32[:, sgt[bi]: sgt[bi] + TM], in_=ps[bg][:, :TM], func=AF.Sigmoid),
                         reads=[("ps", bg)], writes=[("sgt", bi)])
                    P.op("dve", lambda e, by=by, bi=bi: e.tensor_tensor(out=U32[:, tt[bi]: tt[bi] + TM], in0=ps[by][:, :TM], in1=U32[:, sgt[bi]: sgt[bi] + TM], op=ALU.mult),
                         reads=[("ps", by), ("sgt", bi)], writes=[("tt", bi)])
                P.op("pool", lambda e: e.tensor_tensor(out=U32[:, tt[0]: tt[0] + TM], in0=U32[:, tt[0]: tt[0] + TM], in1=U32[:, tt[1]: tt[1] + TM], op=ALU.add),
                     reads=[("tt", 0), ("tt", 1)], writes=[("tt", 0)])
                P.op("pool", lambda e, dc=dc: e.tensor_tensor(out=U[:, mg0 + dc * TM: mg0 + (dc + 1) * TM], in0=U32[:, tt[0]: tt[0] + TM], in1=U32[:, tt[2]: tt[2] + TM], op=ALU.add),
                     reads=[("tt", 0), ("tt", 2)], writes=[("mg", dc)])
            mgk = [("mg", dc) for dc in range(KC)]
            for t in range(KC // 2):
                wt, wk = wtile(j)
                j += 1
                for cc in range(2):
                    dc = 2 * t + cc
                    b = bank()
                    mm_group(ps[b][:, :TM], [(wt[:, kc * 256 + cc * 128: kc * 256 + (cc + 1) * 128], U[:, mg0 + kc * TM: mg0 + (kc + 1) * TM]) for kc in range(KC)],
                             mgk + [wk], ("ps", b))
                    P.op("dve", lambda e, b=b, dc=dc: e.tensor_tensor(out=x[:, dc, c0:c0 + TM], in0=ps[b][:, :TM], in1=x[:, dc, c0:c0 + TM], op=ALU.add),
                         reads=[("ps", b), ("x", dc)], writes=[("x", dc)])
            assert j == c.ffn_base[2]
            P.barrier()

        xkeys = [("x", kc) for kc in range(KC)]
        step = max(1, KC // 4)
        last = None
        for s in range(c.NSEQ):
            memkv(s)
            for i in range(S // T):
                toks = []
                for k0 in range(0, KC, step):
                    toks.append(P.op("act", lambda e, k0=k0, s=s, i=i: e.dma_start(
                        out=x[:, k0:k0 + step, :], in_=xd[s, k0:k0 + step, :, i * T:(i + 1) * T].rearrange("k p t -> p k t")),
                        writes=xkeys[k0:k0 + step], dma="xld"))
                P.retoken([], xkeys, toks, toks[-1])
                ffn(1, c.p_g1)
                rmsnorm_stream(lambda kc: x[:, kc, :], xkeys, lambda kc: h[:, kc, :], [("h", kc) for kc in range(KC)], c.p_gmix, KC, T, 1.0 / c.D)
                for c0 in range(0, T, TM):
                    mixer(s, i, c0)
                ffn(2, c.p_g2)
                toks = []
                for k0 in range(0, KC, step):
                    toks.append(P.op("act", lambda e, k0=k0, s=s, i=i: e.dma_start(
                        out=od[s, k0:k0 + step, :, i * T:(i + 1) * T].rearrange("k p t -> p k t"), in_=x[:, k0:k0 + step, :]),
                        reads=xkeys[k0:k0 + step], dma="xst"))
                P.retoken(xkeys, [], toks, toks[-1])
                last = toks[-1]
        flush_store()
        P.wait_all("act", [last])
        P.emit()
    return nc


_CACHE = {}


def kernel(**inputs):
    cfg = Cfg()
    inp = {k: np.asarray(v) for k, v in inputs.items()}
    ncores = 8
    B = inp["x"].shape[0]
    assert B == ncores * cfg.NSEQ
    wts, sw, prm = pack_weights(inp, cfg)
    x = inp["x"].astype(np.float32, copy=False)
    mem = inp["mem"].astype(np.float32, copy=False)
    in_maps = []
    for r in range(ncores):
        xs = x[r * cfg.NSEQ:(r + 1) * cfg.NSEQ]
        xT = np.ascontiguousarray(xs.transpose(0, 2, 1)).reshape(cfg.NSEQ, cfg.KC, 128, cfg.S)
        ms = mem[r * cfg.NSEQ:(r + 1) * cfg.NSEQ]
        mT = np.ascontiguousarray(ms.transpose(0, 2, 1)).reshape(cfg.NSEQ, cfg.KC, 128, cfg.M)
        in_maps.append({"xT": xT, "memT": mT, "wts": wts, "sw": sw, "prm": prm})
    if "nc" not in _CACHE:
        _CACHE["nc"] = build_program(cfg)
    res = run_bass_kernel_spmd(_CACHE["nc"], in_maps, core_ids=list(range(ncores)))
    out = np.empty((B, cfg.S, cfg.D), np.float32)
    for r in range(ncores):
        oT = np.asarray(res.results[r]["outT"]).reshape(cfg.NSEQ, cfg.D, cfg.S)
        out[r * cfg.NSEQ:(r + 1) * cfg.NSEQ] = oT.transpose(0, 2, 1)
    return out
```

```python
from contextlib import ExitStack
import numpy as np
import concourse.bass as bass
import concourse.mybir as mybir
from concourse.bass_utils import run_bass_kernel_spmd

F32 = mybir.dt.float32
BF16 = mybir.dt.bfloat16
ALU = mybir.AluOpType
AF = mybir.ActivationFunctionType
ENGS = ("pe", "act", "dve", "pool", "sp")
EPS = 1e-6
POOL_WINDOWS = (2, 4, 8, 16)


class Cfg:
    def __init__(self, D=2048, FF=5632, S=2048, NSEQ=2, HF=16, HM=4, M=256, T=512, TM=256, G=22, NS=4):
        self.D, self.FF, self.S, self.NSEQ, self.HF, self.HM, self.M = D, FF, S, NSEQ, HF, HM, M
        self.T, self.TM, self.G, self.NS = T, TM, G, NS
        self.KC = D // 128
        self.FC = FF // 128
        self.FW = HF * 64
        self.MW = HM * 128
        self.NPT = 2
        self.NQT = self.FW // 256
        self.NMT = self.MW // 256
        self.NGRP = self.FC // G
        self.n_ffn = self.FC + self.NGRP * self.KC
        self.n_mix = self.NPT + 3 * self.NQT + self.NMT + 2 * self.KC + self.KC // 2
        self.ffn_base = {1: 0, 2: self.n_ffn + self.n_mix}
        self.mix_base = self.n_ffn
        self.mk_base = 2 * self.n_ffn + self.n_mix
        self.NTILES = self.mk_base + 2 * self.NMT
        gb = self.KC * 128 + 4 * 128 + (HF // 2) * 128 + HM * 128
        self.LINE = max(2 * self.KC * 128, G * 128, self.KC * 256, gb)
        self.LINE += self.LINE % 2
        self.o_pool = 0
        self.o_q = 512
        self.o_k = 512 + self.FW
        self.o_v = 512 + 2 * self.FW
        self.o_f = 512 + 3 * self.FW
        self.o_qm = self.o_f + HF
        self.o_g = self.o_qm + self.MW
        c = 0
        self.p_g1 = c; c += self.KC
        self.p_gmix = c; c += self.KC
        self.p_g2 = c; c += self.KC
        self.p_gmem = c; c += self.KC
        self.p_psc = c; c += 4
        self.p_fq = c; c += 1
        self.p_fk = c; c += 1
        self.p_mq = c; c += 1
        self.p_mk = c; c += 1
        self.p_bf = c; c += HF
        self.p_rc = c; c += 64
        self.NPRM = c
        self.NSW = self.KC * HF + 4 * 128


def pack_weights(inp, cfg):
    KC, FC, G, D, FF = cfg.KC, cfg.FC, cfg.G, cfg.D, cfg.FF
    LINE = cfg.LINE
    wts = np.zeros((cfg.NTILES, 128, LINE), np.float32)

    def ffn_tiles(wgu, wd, base):
        gu = wgu.reshape(KC, 128, 2, FC, 128).transpose(3, 1, 2, 0, 4).reshape(FC, 128, 2 * KC * 128)
        dn = wd.reshape(cfg.NGRP, G, 128, KC, 128).transpose(0, 3, 2, 1, 4).reshape(cfg.NGRP, KC, 128, G * 128)
        j = base
        for grp in range(cfg.NGRP):
            wts[j:j + G, :, :2 * KC * 128] = gu[grp * G:(grp + 1) * G]
            j += G
            wts[j:j + KC, :, :G * 128] = dn[grp]
            j += KC

    ffn_tiles(inp["ffn1_w_gate_up"][0], inp["ffn1_w_down"][0], cfg.ffn_base[1])
    ffn_tiles(inp["ffn2_w_gate_up"][0], inp["ffn2_w_down"][0], cfg.ffn_base[2])
    win = inp["w_in"][0]

    def in_tile(c0):
        return win[:, c0:c0 + 256].reshape(KC, 128, 256).transpose(1, 0, 2).reshape(128, KC * 256)

    j = cfg.mix_base
    for off, n in ((cfg.o_pool, cfg.NPT), (cfg.o_q, cfg.NQT), (cfg.o_k, cfg.NQT), (cfg.o_v, cfg.NQT), (cfg.o_qm, cfg.NMT)):
        for t in range(n):
            wts[j, :, :KC * 256] = in_tile(off + 256 * t)
            j += 1
    wpu, wfo, wmo = inp["w_pool_up"][0], inp["w_fox_o"][0], inp["w_mem_o"][0]
    for dc in range(KC):
        ga = np.stack([win[:, cfg.o_g + b * D + dc * 128: cfg.o_g + b * D + (dc + 1) * 128] for b in (0, 1)])
        wts[j, :, :2 * KC * 128] = ga.reshape(2, KC, 128, 128).transpose(2, 0, 1, 3).reshape(128, -1)
        j += 1
        gm = win[:, cfg.o_g + 2 * D + dc * 128: cfg.o_g + 2 * D + (dc + 1) * 128]
        parts = [gm.reshape(KC, 128, 128).transpose(1, 0, 2).reshape(128, -1)]
        for w in (wpu, wfo, wmo):
            kk = w.shape[0] // 128
            parts.append(w[:, dc * 128:(dc + 1) * 128].reshape(kk, 128, 128).transpose(1, 0, 2).reshape(128, -1))
        gb = np.concatenate(parts, axis=1)
        wts[j, :, :gb.shape[1]] = gb
        j += 1
    wout = inp["w_out"][0]
    for t in range(KC // 2):
        wts[j, :, :KC * 256] = wout[:, t * 256:(t + 1) * 256].reshape(KC, 128, 256).transpose(1, 0, 2).reshape(128, -1)
        j += 1
    assert j == cfg.ffn_base[2]
    wkv = inp["w_mem_kv"][0]
    j = cfg.mk_base
    for t in range(2 * cfg.NMT):
        wts[j, :, :KC * 256] = wkv[:, t * 256:(t + 1) * 256].reshape(KC, 128, 256).transpose(1, 0, 2).reshape(128, -1)
        j += 1
    sw = np.zeros((128, cfg.NSW), np.float32)
    sw[:, :KC * cfg.HF] = win[:, cfg.o_f:cfg.o_f + cfg.HF].reshape(KC, 128, cfg.HF).transpose(1, 0, 2).reshape(128, -1)
    sw[:, KC * cfg.HF:] = inp["pool_w"][0].transpose(1, 0, 2).reshape(128, 4 * 128)
    prm = np.zeros((128, cfg.NPRM), np.float32)
    col = lambda v: np.asarray(v, np.float32).reshape(-1, 128).T
    prm[:, cfg.p_g1:cfg.p_g1 + KC] = col(inp["ffn1_norm"][0])
    prm[:, cfg.p_gmix:cfg.p_gmix + KC] = col(inp["mix_norm"][0])
    prm[:, cfg.p_g2:cfg.p_g2 + KC] = col(inp["ffn2_norm"][0])
    prm[:, cfg.p_gmem:cfg.p_gmem + KC] = col(inp["mem_norm"][0])
    prm[:, cfg.p_psc:cfg.p_psc + 4] = col(inp["pool_scale"][0])
    prm[:, cfg.p_fq] = np.tile(np.asarray(inp["fox_q_norm"][0], np.float32), 2)
    prm[:, cfg.p_fk] = np.tile(np.asarray(inp["fox_k_norm"][0], np.float32), 2)
    prm[:, cfg.p_mq] = np.asarray(inp["mem_q_norm"][0], np.float32)
    prm[:, cfg.p_mk] = np.asarray(inp["mem_k_norm"][0], np.float32)
    prm[:, cfg.p_bf:cfg.p_bf + cfg.HF] = np.asarray(inp["b_forget"][0], np.float32)[None, :]
    for g, w in enumerate(POOL_WINDOWS):
        prm[:, cfg.p_rc + 16 * g: cfg.p_rc + 16 * g + 16] = (1.0 / np.minimum(np.arange(1, 17), w))[None, :]
    return wts, sw, prm


class Prog:
    def __init__(self, nc, stack):
        self.nc, self.stack = nc, stack
        self.ops = {e: [] for e in ENGS}
        self.count = {e: 0 for e in ENGS}
        self.seen = {e: {} for e in ENGS}
        self.sem = {}
        self.dcount = {}
        self.buf = {}
        self.alias = {}

    def _expand(self, keys):
        out = []
        for k in keys:
            out.append(k)
            out.extend(self.alias.get(k, ()))
        return out

    def _sem(self, src):
        if src not in self.sem:
            self.sem[src] = self.stack.enter_context(self.nc.semaphore("s_" + src.replace(":", "_")))
        return self.sem[src]

    def op(self, eng, fn, reads=(), writes=(), signal=True, dma=None):
        reads, writes = self._expand(reads), self._expand(writes)
        deps = {}

        def add(tok):
            if tok is not None and deps.get(tok[0], 0) < tok[1]:
                deps[tok[0]] = tok[1]

        for k in reads:
            st = self.buf.get(k)
            if st is not None:
                add(st[0])
        for k in writes:
            st = self.buf.get(k)
            if st is not None:
                add(st[0])
                for t in st[1]:
                    add(t)
        waits = []
        seen = self.seen[eng]
        for s, v in deps.items():
            if seen.get(s, 0) >= v or (s == "pe" and eng == "pe"):
                continue
            seen[s] = v
            waits.append((s, v))
        if dma is not None:
            src = "dma:" + dma
            self.dcount[src] = self.dcount.get(src, 0) + 16
            tok, inc = (src, self.dcount[src]), (src, 16)
        elif signal:
            self.count[eng] += 1
            tok, inc = (eng, self.count[eng]), (eng, 1)
        else:
            tok, inc = (eng, self.count[eng] + 1), None
        self.ops[eng].append((waits, fn, inc))
        for k in reads:
            self.buf.setdefault(k, [None, []])[1].append(tok)
        for k in writes:
            self.buf[k] = [tok, []]
        return tok

    def retoken(self, keys_r, keys_w, old_toks, tok):
        olds = set(old_toks)
        for k in keys_r:
            st = self.buf[k]
            st[1] = [t for t in st[1] if t not in olds] + [tok]
        for k in keys_w:
            self.buf[k] = [tok, []]

    def barrier(self, extra=()):
        for e in ("pe", "act", "dve", "pool"):
            waits = []
            for s in ("pe", "act", "dve", "pool"):
                v = self.count[s]
                if s != e and v > self.seen[e].get(s, 0):
                    self.seen[e][s] = v
                    waits.append((s, v))
            for s, v in extra:
                if v > self.seen[e].get(s, 0):
                    self.seen[e][s] = v
                    waits.append((s, v))
            self.ops[e].append((waits, None, None))

    def wait_all(self, eng, toks):
        self.ops[eng].append((list(toks), None, None))

    def emit(self):
        nc, ops, sem = self.nc, self.ops, self._sem
        for s_ in list(self.dcount) + ["pe", "act", "dve", "pool"]:
            sem(s_)

        def run(engine, lst):
            for waits, fn, inc in lst:
                for s, v in waits:
                    engine.wait_ge(sem(s), v)
                if fn is None:
                    continue
                ins = fn(engine)
                if inc is not None:
                    ins.then_inc(sem(inc[0]), inc[1])

        with nc.Block() as block:
            @block.tensor
            def _(e):
                run(e, ops["pe"])

            @block.scalar
            def _(e):
                run(e, ops["act"])

            @block.vector
            def _(e):
                run(e, ops["dve"])

            @block.gpsimd
            def _(e):
                run(e, ops["pool"])

            @block.sync
            def _(e):
                run(e, ops["sp"])


def build_program(cfg, prepass=True):
    c = cfg
    KC, FC, G, T, TM, HF, HM, M, S, NS, LINE = c.KC, c.FC, c.G, c.T, c.TM, c.HF, c.HM, c.M, c.S, c.NS, c.LINE
    NJ = S // 128
    HP = HF // 2
    MC = M // 128
    H2 = LINE // 2
    nc = bass.Bass("TRN2", target_bir_lowering=False)
    xd = nc.dram_tensor("xT", [c.NSEQ, KC, 128, S], F32, kind="ExternalInput").ap()
    md = nc.dram_tensor("memT", [c.NSEQ, KC, 128, M], F32, kind="ExternalInput").ap()
    wts = nc.dram_tensor("wts", [c.NTILES, 128, LINE], F32, kind="ExternalInput").ap()
    swd = nc.dram_tensor("sw", [128, c.NSW], F32, kind="ExternalInput").ap()
    prd = nc.dram_tensor("prm", [128, c.NPRM], F32, kind="ExternalInput").ap()
    od = nc.dram_tensor("outT", [c.NSEQ, KC, 128, S], F32, kind="ExternalOutput").ap()
    wtb = nc.dram_tensor("wtb", [c.NTILES, 128, LINE], BF16).ap()

    with ExitStack() as st:
        P = Prog(nc, st)
        sb = lambda name, shape, dt: st.enter_context(nc.sbuf_tensor(name, shape, dt))
        x = sb("x", [128, KC, T], F32)
        h = sb("h", [128, KC, T], BF16)
        kT = sb("kT", [128, HP, S], BF16)
        Vc = sb("Vc", [128, NJ * c.FW], BF16)
        Vc32 = Vc.bitcast(F32)
        km = sb("km", [128, HM, M], BF16)
        vm = sb("vm", [128, MC, c.MW], BF16)
        ring = [sb("ring%d" % i, [128, LINE], BF16) for i in range(NS)]
        sq = [sb("sq%d" % i, [128, T], BF16) for i in range(2)]
        sd = sb("sd", [128, T], F32)
        rstd = sb("rstd", [128, T], F32)
        NQN = max(TM, M)
        sdq = [sb("sdq%d" % i, [128, NQN], F32) for i in range(2)]
        rstdq = [sb("rstdq%d" % i, [128, NQN], F32) for i in range(2)]
        ones_bf = sb("ones_bf", [128, 128], BF16)
        blk64 = sb("blk64", [128, 128], BF16)
        tri = sb("tri", [128, 128], F32)
        tri_bf = sb("tri_bf", [128, 128], BF16)
        ones_f = sb("ones_f", [128, 128], F32)
        prm = sb("prm_sb", [128, c.NPRM], F32)
        swb = sb("swb", [128, c.NSW], BF16)
        Cneg = sb("Cneg", [128, NJ, HF], F32)
        carry = sb("carry", [128, NJ + 1, HF], F32)
        NQ = TM // 128
        biast = sb("biast", [128, NJ, HF], F32)
        zt = sb("zt", [128, 3, HF], F32)
        uhist = sb("uhist", [128, 4, 16], F32)
        ffn_b = G * T * 2 + 2 * T * 4
        W16 = 16 + TM
        mix_b = (4 * W16 * 4 + 2 * W16 * 4 + 4 * TM * 2 + 4 * TM * 2 + HP * TM * 2 + HM * TM * 2 + HP * TM * 2
                 + HM * TM * 2 + 3 * TM * 2 + 2 * TM * 4 + 3 * TM * 4 + 3 * TM * 4 + KC * TM * 2 + 64 * 4)
        mem_b = KC * M * 4 + KC * M * 2
        pre_b = 3 * H2 * 4 + 3 * H2 * 2
        UB = max(ffn_b, mix_b, mem_b, pre_b)
        UB += (-UB) % 64
        U = sb("U", [128, UB // 2], BF16)
        U32 = U.bitcast(F32)

        class Carve:
            def __init__(self):
                self.off = 0

            def bf(self, n):
                assert self.off % 4 == 0
                a = self.off // 2
                self.off += n * 2
                self.off += (-self.off) % 4
                assert self.off <= UB
                return a, a + n

            def f32(self, n):
                a = self.off // 4
                self.off += n * 4
                assert self.off <= UB
                return a, a + n

        ps = [st.enter_context(nc.psum_tensor("ps%d" % i, [128, 512], F32)) for i in range(8)]
        rr = {"i": 0}

        def bank(pool=(0, 1, 2, 3, 4, 5, 6, 7)):
            rr["i"] += 1
            return pool[rr["i"] % len(pool)]

        pcol = lambda cidx: prm[:, cidx:cidx + 1]

        P.op("sp", lambda e: e.dma_start(out=prm[:, :], in_=prd), writes=["prm"], dma="prm")
        P.op("sp", lambda e: e.dma_start(out=U32[:, 0:c.NSW], in_=swd), writes=[("sbb", 0)], dma="swf")
        P.op("dve", lambda e: e.memset(ones_bf[:, :], 1.0), writes=["ones_bf"])
        P.op("dve", lambda e: e.memset(ones_f[:, :], 1.0), writes=["ones_f"])
        P.op("dve", lambda e: e.memset(blk64[:, :], 0.0), writes=["blk64"])
        P.op("dve", lambda e: e.memset(blk64[0:64, 0:64], 1.0), writes=["blk64"])
        P.op("dve", lambda e: e.memset(blk64[64:128, 64:128], 1.0), writes=["blk64"])
        P.op("pool", lambda e: e.affine_select(out=tri[:, :], in_=ones_f[:, :], pattern=[[1, 128]], compare_op=ALU.is_ge,
                                               fill=0.0, base=0, channel_multiplier=-1), reads=["ones_f"], writes=["tri"])
        P.op("dve", lambda e: e.tensor_copy(out=tri_bf[:, :], in_=tri[:, :]), reads=["tri"], writes=["tri_bf"])
        P.op("dve", lambda e: e.tensor_copy(out=swb[:, :], in_=U32[:, 0:c.NSW]), reads=[("sbb", 0)], writes=["swb"])
        P.barrier()
        wf_ap = lambda kc: swb[:, kc * HF:(kc + 1) * HF]
        poolw_ap = lambda g: swb[:, KC * HF + g * 128: KC * HF + (g + 1) * 128]

        if prepass:
            cv = Carve()
            NSF, NSB, PLA = 4, 4, 3
            assert NSF * H2 <= NJ * c.FW // 2 or True
            if NSF * H2 <= NJ * c.FW // 2:
                sf_ap = lambda b_: Vc32[:, b_ * H2:(b_ + 1) * H2]
            else:
                sft = sb("sft", [128, NSF * H2], F32)
                sf_ap = lambda b_: sft[:, b_ * H2:(b_ + 1) * H2]
            sbb = [cv.bf(H2) for _ in range(NSB)]
            ceng = ("dve", "act")
            chunks = [(j, hf) for j in range(c.NTILES) for hf in range(2)]
            NCH = len(chunks)

            def pre_load(n):
                j, hf = chunks[n]
                b = n % NSF
                src = wts[j, :, hf * H2:(hf + 1) * H2]
                P.op("sp", lambda e: e.dma_start(out=sf_ap(b), in_=src), writes=[("sf", b)], dma="sf%d" % b)

            def pre_cast_store(n):
                j, hf = chunks[n]
                b = n % NSF
                b2 = n % NSB
                ba, bb = sbb[b2]
                dst = wtb[j, :, hf * H2:(hf + 1) * H2]
                ce = ceng[n % 2]
                if ce == "act":
                    P.op("act", lambda e: e.activation(out=U[:, ba:bb], in_=sf_ap(b), func=AF.Copy), reads=[("sf", b)], writes=[("sbb", b2)])
                else:
                    P.op(ce, lambda e: e.tensor_copy(out=U[:, ba:bb], in_=sf_ap(b)), reads=[("sf", b)], writes=[("sbb", b2)])
                P.op("sp", lambda e: e.dma_start(out=dst, in_=U[:, ba:bb]), reads=[("sbb", b2)], dma="sb%d" % b2)

            for n in range(NCH + PLA):
                if n < NCH:
                    pre_load(n)
                if n >= PLA:
                    pre_cast_store(n - PLA)
            extra = [(s_, v) for s_, v in P.dcount.items() if s_.startswith("dma:s")]
            P.wait_all("sp", extra)
            P.barrier(extra)

        wn = {"n": 0}

        fresh_mode = not prepass
        done = set()
        pending = {"s": None}
        stn = {"n": 0}
        PL = LINE // 4
        base32 = (T // 128) * c.FW // 2
        NSTG = 4
        if fresh_mode:
            if (NJ * c.FW // 2 - base32) // PL >= 4:
                NSTG = (NJ * c.FW // 2 - base32) // PL
                stage_ap = lambda b_: Vc32[:, base32 + b_ * PL: base32 + (b_ + 1) * PL]
            else:
                stg_t = sb("stg_t", [128, 4 * PL], F32)
                stage_ap = lambda b_: stg_t[:, b_ * PL:(b_ + 1) * PL]
        cast_eng = ("dve", "act", "pool", "pool")
        for sl_ in range(NS):
            P.alias[("ring", sl_)] = [("ringq", sl_, q_) for q_ in range(4)]

        def flush_store():
            if pending["s"] is not None:
                j_, slot_ = pending["s"]
                pending["s"] = None
                P.op("sp", lambda e: e.dma_start(out=wtb[j_, :, :], in_=ring[slot_][:, :]), reads=[("ring", slot_)], writes=[("wtb", j_)], dma="ws%d" % slot_)

        def wtile(j):
            slot = wn["n"] % NS
            wn["n"] += 1
            if fresh_mode and j not in done:
                done.add(j)
                for q in range(4):
                    b_ = stn["n"] % NSTG
                    stn["n"] += 1
                    sa = stage_ap(b_)
                    src = wts[j, :, q * PL:(q + 1) * PL]
                    P.op("sp", lambda e, sa=sa, src=src: e.dma_start(out=sa, in_=src), writes=[("stg", b_)], dma="stg%d" % b_)
                    ce = cast_eng[q]
                    if ce == "act":
                        P.op("act", lambda e, sa=sa, q=q: e.activation(out=ring[slot][:, q * PL:(q + 1) * PL], in_=sa, func=AF.Copy),
                             reads=[("stg", b_)], writes=[("ringq", slot, q)])
                    else:
                        P.op(ce, lambda e, sa=sa, q=q: e.tensor_copy(out=ring[slot][:, q * PL:(q + 1) * PL], in_=sa),
                             reads=[("stg", b_)], writes=[("ringq", slot, q)])
                flush_store()
                pending["s"] = (j, slot)
            else:
                flush_store()
                P.op("sp", lambda e: e.dma_start(out=ring[slot][:, :], in_=wtb[j, :, :]), reads=[("wtb", j)], writes=[("ring", slot)], dma="w%d" % slot)
            return ring[slot], ("ring", slot)

        def mm_group(out_ap, pairs, reads, bankkey, start=True, stop=True, each=None):
            n = len(pairs)
            for i_, (l, r) in enumerate(pairs):
                rd_ = reads if each is None else list(reads) + [each[i_]]
                P.op("pe", lambda e, l=l, r=r, i_=i_: e.matmul(out_ap, lhsT=l, rhs=r, start=(start and i_ == 0), stop=(stop and i_ == n - 1)),
                     reads=rd_, writes=[bankkey], signal=(i_ == n - 1))

        def norm_finish(ssq_bank, n, inv_n):
            P.op("act", lambda e: e.activation(out=sd[:, :n], in_=ps[ssq_bank][:, :n], func=AF.Sqrt, bias=EPS, scale=inv_n),
                 reads=[("ps", ssq_bank)], writes=["sd"])
            P.op("dve", lambda e: e.reciprocal(out=rstd[:, :n], in_=sd[:, :n]), reads=["sd"], writes=["rstd"])

        def rmsnorm_stream(src_fn, src_keys, dst_fn, dst_keys, gcol0, nk, n, inv_n):
            bs = bank()
            for kc in range(nk):
                b = kc % 2
                P.op("act", lambda e, kc=kc, b=b: e.activation(out=sq[b][:, :n], in_=src_fn(kc), func=AF.Square),
                     reads=[src_keys[kc]], writes=[("sq", b)])
                P.op("pe", lambda e, kc=kc, b=b: e.matmul(ps[bs][:, :n], lhsT=ones_bf[:, :], rhs=sq[b][:, :n], start=(kc == 0), stop=(kc == nk - 1)),
                     reads=[("sq", b), "ones_bf"], writes=[("ps", bs)], signal=True)
            norm_finish(bs, n, inv_n)
            for kc in range(nk):
                P.op("dve", lambda e, kc=kc: e.scalar_tensor_tensor(out=dst_fn(kc), in0=src_fn(kc), scalar=pcol(gcol0 + kc), in1=rstd[:, :n],
                                                                  op0=ALU.mult, op1=ALU.mult),
                     reads=[src_keys[kc], "rstd", "prm"], writes=[dst_keys[kc]])

        def qknorm_a(src_bank, n, k):
            P.op("act", lambda e: e.activation(out=sq[k][:, :n], in_=ps[src_bank][:, :n], func=AF.Square),
                 reads=[("ps", src_bank)], writes=[("sq", k)])

        def qknorm_b(src_bank, n, k, ones_ap, ones_key, inv_n, gcol, dst_ap, dst_key):
            b = bank()
            P.op("pe", lambda e: e.matmul(ps[b][:, :n], lhsT=ones_ap, rhs=sq[k][:, :n], start=True, stop=True),
                 reads=[("sq", k), ones_key], writes=[("ps", b)])
            P.op("act", lambda e: e.activation(out=sdq[k][:, :n], in_=ps[b][:, :n], func=AF.Sqrt, bias=EPS, scale=inv_n),
                 reads=[("ps", b)], writes=[("sdq", k)])
            P.op("dve", lambda e: e.reciprocal(out=rstdq[k][:, :n], in_=sdq[k][:, :n]), reads=[("sdq", k)], writes=[("rstdq", k)])
            P.op("dve", lambda e: e.scalar_tensor_tensor(out=dst_ap, in0=ps[src_bank][:, :n], scalar=pcol(gcol), in1=rstdq[k][:, :n],
                                                         op0=ALU.mult, op1=ALU.mult),
                 reads=[("ps", src_bank), ("rstdq", k), "prm"], writes=[dst_key])

        def qknorm(src_bank, n, ones_ap, ones_key, inv_n, gcol, dst_ap, dst_key):
            qknorm_a(src_bank, n, 0)
            qknorm_b(src_bank, n, 0, ones_ap, ones_key, inv_n, gcol, dst_ap, dst_key)

        def ffn(idx, gcol0):
            cv = Carve()
            act_a, _ = cv.bf(G * T)
            sg = [cv.f32(T) for _ in range(2)]
            rmsnorm_stream(lambda kc: x[:, kc, :], [("x", kc) for kc in range(KC)],
                           lambda kc: h[:, kc, :], [("h", kc) for kc in range(KC)], gcol0, KC, T, 1.0 / c.D)
            hkeys = [("h", kc) for kc in range(KC)]
            j = c.ffn_base[idx]
            for grp in range(c.NGRP):
                for ff in range(G):
                    wt, wk = wtile(j)
                    j += 1
                    bg, bu = bank(), bank()
                    mm_group(ps[bg][:, :T], [(wt[:, kc * 128:(kc + 1) * 128], h[:, kc, :]) for kc in range(KC)], [wk], ("ps", bg), each=hkeys)
                    mm_group(ps[bu][:, :T], [(wt[:, (KC + kc) * 128:(KC + kc + 1) * 128], h[:, kc, :]) for kc in range(KC)], [wk], ("ps", bu), each=hkeys)
                    s0, s1 = sg[ff % 2]
                    P.op("act", lambda e, bg=bg, s0=s0, s1=s1: e.activation(out=U32[:, s0:s1], in_=ps[bg][:, :T], func=AF.Silu),
                         reads=[("ps", bg)], writes=[("sg", ff % 2)])
                    a0 = act_a + ff * T
                    P.op("dve", lambda e, bu=bu, s0=s0, s1=s1, a0=a0: e.tensor_tensor(out=U[:, a0:a0 + T], in0=ps[bu][:, :T], in1=U32[:, s0:s1], op=ALU.mult),
                         reads=[("ps", bu), ("sg", ff % 2)], writes=[("act", ff)])
                akeys = [("act", ff) for ff in range(G)]
                for dc in range(KC):
                    wt, wk = wtile(j)
                    j += 1
                    bd = bank()
                    mm_group(ps[bd][:, :T], [(wt[:, kk * 128:(kk + 1) * 128], U[:, act_a + kk * T: act_a + (kk + 1) * T]) for kk in range(G)],
                             [wk], ("ps", bd), each=akeys)
                    P.op("dve", lambda e, bd=bd, dc=dc: e.scalar_tensor_tensor(out=x[:, dc, :], in0=ps[bd][:, :T], scalar=0.5, in1=x[:, dc, :],
                                                                             op0=ALU.mult, op1=ALU.add),
                         reads=[("ps", bd), ("x", dc)], writes=[("x", dc)])
            P.barrier()

        def memkv(s):
            cv = Carve()
            mf0, _ = cv.f32(KC * M)
            hm0, _ = cv.bf(KC * M)
            toks = []
            step = max(1, KC // 4)
            for k0 in range(0, KC, step):
                toks.append(P.op("act", lambda e, k0=k0: e.dma_start(
                    out=U32[:, mf0 + k0 * M: mf0 + (k0 + step) * M].rearrange("p (k m) -> p k m", k=step),
                    in_=md[s, k0:k0 + step, :, :].rearrange("k p m -> p k m")), writes=["memf"], dma="mem"))
            P.retoken([], ["memf"], toks, toks[-1])
            mfk = ["memf"] * KC
            rmsnorm_stream(lambda kc: U32[:, mf0 + kc * M: mf0 + (kc + 1) * M], mfk,
                           lambda kc: U[:, hm0 + kc * M: hm0 + (kc + 1) * M], [("hm", kc) for kc in range(KC)], c.p_gmem, KC, M, 1.0 / c.D)
            hmk = [("hm", kc) for kc in range(KC)]
            hm_ap = lambda kc, a, b: U[:, hm0 + kc * M + a: hm0 + kc * M + b]
            for t in range(c.NMT):
                wt, wk = wtile(c.mk_base + t)
                for cc in range(2):
                    hd = 2 * t + cc
                    b = bank()
                    mm_group(ps[b][:, :M], [(wt[:, kc * 256 + cc * 128: kc * 256 + (cc + 1) * 128], hm_ap(kc, 0, M)) for kc in range(KC)],
                             hmk + [wk], ("ps", b))
                    qknorm(b, M, ones_bf[:, :], "ones_bf", 1.0 / 128, c.p_mk, km[:, hd, :], ("km", hd))
            for t in range(c.NMT):
                wt, wk = wtile(c.mk_base + c.NMT + t)
                for mc in range(MC):
                    b = bank()
                    mm_group(ps[b][:, :256], [(hm_ap(kc, mc * 128, (mc + 1) * 128), wt[:, kc * 256:(kc + 1) * 256]) for kc in range(KC)],
                             hmk + [wk], ("ps", b))
                    P.op("act", lambda e, b=b, mc=mc, t=t: e.activation(out=vm[:, mc, t * 256:(t + 1) * 256], in_=ps[b][:, :256], func=AF.Copy),
                         reads=[("ps", b)], writes=[("vm", mc)])
            P.barrier()

        def mixer(s, i, c0):
            t0 = i * T + c0
            jq0 = t0 // 128
            cv = Carve()
            ub0, _ = cv.f32(4 * W16)
            lv = [cv.f32(W16)[0] for _ in range(2)]
            t16, _ = cv.f32(64)
            db0, _ = cv.bf(4 * TM)
            mx0, _ = cv.bf(4 * TM)
            q0, _ = cv.bf(HP * TM)
            qm0, _ = cv.bf(HM * TM)
            at0, _ = cv.bf(HP * TM)
            mo0, _ = cv.bf(HM * TM)
            pt = [cv.bf(TM)[0] for _ in range(3)]
            rd = [cv.f32(TM)[0] for _ in range(2)]
            sgt = [cv.f32(TM)[0] for _ in range(3)]
            tt = [cv.f32(TM)[0] for _ in range(3)]
            mg0, _ = cv.bf(KC * TM)
            hc = lambda kc: h[:, kc, c0:c0 + TM]
            hkeys = [("h", kc) for kc in range(KC)]
            ub = lambda g, a, b: U32[:, ub0 + g * W16 + a: ub0 + g * W16 + b]
            j = c.mix_base

            for g in range(4):
                if t0 == 0:
                    P.op("pool", lambda e, g=g: e.memset(ub(g, 0, 16), 0.0), writes=[("ub", g)])
                else:
                    P.op("pool", lambda e, g=g: e.tensor_copy(out=ub(g, 0, 16), in_=uhist[:, g, :]), reads=[("uhist", g)], writes=[("ub", g)])
            for t in range(c.NPT):
                wt, wk = wtile(j)
                j += 1
                for cc in range(2):
                    g = 2 * t + cc
                    b = bank()
                    mm_group(ps[b][:, :TM], [(wt[:, kc * 256 + cc * 128: kc * 256 + (cc + 1) * 128], hc(kc)) for kc in range(KC)], hkeys + [wk], ("ps", b))
                    P.op("act", lambda e, b=b, g=g: e.activation(out=ub(g, 16, W16), in_=ps[b][:, :TM], func=AF.Copy),
                         reads=[("ps", b)], writes=[("ub", g)])
            for g in range(4):
                w = POOL_WINDOWS[g]
                cur = lambda a, b, g=g: ub(g, a, b)
                curk = ("ub", g)
                for lvl in range(g + 1):
                    sh = 1 << lvl
                    dst0 = lv[lvl % 2]
                    dst = lambda a, b, dst0=dst0: U32[:, dst0 + a: dst0 + b]
                    P.op("pool", lambda e, cur=cur, dst=dst, sh=sh: e.tensor_tensor(out=dst(sh, W16), in0=cur(sh, W16), in1=cur(0, W16 - sh), op=ALU.add),
                         reads=[curk], writes=[("lv", lvl % 2)])
                    cur, curk = dst, ("lv", lvl % 2)
                P.op("dve", lambda e, cur=cur, g=g, w=w: e.scalar_tensor_tensor(out=U[:, db0 + g * TM: db0 + (g + 1) * TM], in0=cur(16, W16), scalar=1.0 / w,
                                                                              in1=ub(g, 16, W16), op0=ALU.mult, op1=ALU.subtract),
                     reads=[curk, ("ub", g)], writes=[("db", g)])
                if t0 == 0:
                    P.op("dve", lambda e, cur=cur, g=g: e.tensor_tensor(out=U32[:, t16 + 16 * g: t16 + 16 * g + 16], in0=cur(16, 32),
                                                                      in1=prm[:, c.p_rc + 16 * g: c.p_rc + 16 * g + 16], op=ALU.mult),
                         reads=[curk, "prm"], writes=[("t16", g)])
                    P.op("dve", lambda e, g=g: e.tensor_tensor(out=U[:, db0 + g * TM: db0 + g * TM + 16], in0=U32[:, t16 + 16 * g: t16 + 16 * g + 16],
                                                             in1=ub(g, 16, 32), op=ALU.subtract),
                         reads=[("t16", g), ("ub", g)], writes=[("db", g)])
                b = bank()
                P.op("pe", lambda e, b=b, g=g: e.matmul(ps[b][:, :TM], lhsT=poolw_ap(g), rhs=U[:, db0 + g * TM: db0 + (g + 1) * TM], start=True, stop=True),
                     reads=[("db", g), "swb"], writes=[("ps", b)])
                P.op("dve", lambda e, b=b, g=g: e.tensor_scalar(out=U[:, mx0 + g * TM: mx0 + (g + 1) * TM], in0=ps[b][:, :TM], scalar1=pcol(c.p_psc + g),
                                                              scalar2=None, op0=ALU.mult),
                     reads=[("ps", b), "prm"], writes=[("mx", g)])
                P.op("pool", lambda e, g=g: e.tensor_copy(out=uhist[:, g, :], in_=ub(g, TM, TM + 16)), reads=[("ub", g)], writes=[("uhist", g)])

            jb_q = c.mix_base + c.NPT
            jobs = []
            for t in range(c.NQT):
                for cc in range(2):
                    hp = 2 * t + cc
                    jobs.append((jb_q + t, cc, blk64[:, :], "blk64", 1.0 / 64, c.p_fq, U[:, q0 + hp * TM: q0 + (hp + 1) * TM], ("q", hp)))
            for t in range(c.NQT):
                for cc in range(2):
                    hp = 2 * t + cc
                    jobs.append((jb_q + c.NQT + t, cc, blk64[:, :], "blk64", 1.0 / 64, c.p_fk, kT[:, hp, t0:t0 + TM], ("kT", hp)))
            for t in range(c.NMT):
                for cc in range(2):
                    hd = 2 * t + cc
                    jobs.append((jb_q + 3 * c.NQT + t, cc, ones_bf[:, :], "ones_bf", 1.0 / 128, c.p_mq, U[:, qm0 + hd * TM: qm0 + (hd + 1) * TM], ("qm", hd)))
            pend = None
            cur_w = None
            for jn, (jt, cc, ones_ap, ones_key, inv_n, gcol, dst_ap, dst_key) in enumerate(jobs):
                if cc == 0:
                    cur_w = wtile(jt)
                wt, wk = cur_w
                b = bank()
                mm_group(ps[b][:, :TM], [(wt[:, kc * 256 + cc * 128: kc * 256 + (cc + 1) * 128], hc(kc)) for kc in range(KC)], hkeys + [wk], ("ps", b))
                qknorm_a(b, TM, jn % 2)
                if pend is not None:
                    qknorm_b(*pend)
                pend = (b, TM, jn % 2, ones_ap, ones_key, inv_n, gcol, dst_ap, dst_key)
            qknorm_b(*pend)
            j = jb_q + 2 * c.NQT
            for t in range(c.NQT):
                wt, wk = wtile(j)
                j += 1
                for tc in range(NQ):
                    b = bank()
                    mm_group(ps[b][:, :256], [(h[:, kc, c0 + tc * 128: c0 + (tc + 1) * 128], wt[:, kc * 256:(kc + 1) * 256]) for kc in range(KC)],
                             hkeys + [wk], ("ps", b))
                    P.op("act", lambda e, b=b, tc=tc, t=t: e.activation(out=Vc[:, (jq0 + tc) * c.FW + t * 256:(jq0 + tc) * c.FW + (t + 1) * 256], in_=ps[b][:, :256], func=AF.Copy),
                         reads=[("ps", b)], writes=[("Vc", jq0 + tc)])
            if t0 == 0:
                P.op("dve", lambda e: e.memset(carry[:, 0, :], 0.0), writes=[("carry", 0)])
            for tc in range(NQ):
                jj = jq0 + tc
                b = bank()
                mm_group(ps[b][:, :HF], [(h[:, kc, c0 + tc * 128: c0 + (tc + 1) * 128], wf_ap(kc)) for kc in range(KC)], hkeys + ["swb"], ("ps", b))
                P.op("dve", lambda e, b=b: e.tensor_tensor(out=zt[:, 0, :], in0=ps[b][:, :HF], in1=prm[:, c.p_bf:c.p_bf + HF], op=ALU.add),
                     reads=[("ps", b), "prm"], writes=["z0"])
                P.op("act", lambda e: e.activation(out=zt[:, 1, :], in_=zt[:, 0, :], func=AF.Exp, scale=-1.0), reads=["z0"], writes=["z1"])
                P.op("act", lambda e: e.activation(out=zt[:, 2, :], in_=zt[:, 1, :], func=AF.Ln, bias=1.0, scale=1.0), reads=["z1"], writes=["z2"])
                b1, b2 = bank(), bank()
                P.op("pe", lambda e, b1=b1: e.matmul(ps[b1][:, :HF], lhsT=tri[:, :], rhs=zt[:, 2, :], start=True, stop=True),
                     reads=["z2", "tri"], writes=[("ps", b1)])
                P.op("pe", lambda e, b2=b2: e.matmul(ps[b2][:, :HF], lhsT=ones_f[:, :], rhs=zt[:, 2, :], start=True, stop=True),
                     reads=["z2", "ones_f"], writes=[("ps", b2)])
                P.op("dve", lambda e, b1=b1, jj=jj: e.tensor_tensor(out=Cneg[:, jj, :], in0=ps[b1][:, :HF], in1=carry[:, jj, :], op=ALU.add),
                     reads=[("ps", b1), ("carry", jj)], writes=[("Cneg", jj)])
                P.op("dve", lambda e, b2=b2, jj=jj: e.tensor_tensor(out=carry[:, jj + 1, :], in0=ps[b2][:, :HF], in1=carry[:, jj, :], op=ALU.add),
                     reads=[("ps", b2), ("carry", jj)], writes=[("carry", jj + 1)])
            for jk in range(jq0 + NQ):
                P.op("dve", lambda e, jk=jk: e.tensor_tensor(out=biast[:, jk, :], in0=Cneg[:, jk, :], in1=carry[:, jq0 + 1, :], op=ALU.subtract),
                     reads=[("Cneg", jk), ("carry", jq0 + 1)], writes=[("bias", jk)])
            j = c.mix_base + c.NPT + 3 * c.NQT + c.NMT

            nk = jq0 + NQ
            assert NQ <= 2
            spool = (0, 1, 2, 3)
            its = [("f", hp, hh, jk) for hp in range(HP) for hh in range(2) for jk in range(nk)]
            its += [("m", hd, 0, mc) for hd in range(HM) for mc in range(MC)]

            def it_qk(n):
                kind, a_, hh, jk = its[n]
                bs = spool[n % 4]
                if kind == "f":
                    hp, r0 = a_, 64 * hh
                    cq = max(0, jk - jq0) * 128
                    P.op("pe", lambda e: e.matmul(ps[bs][:, cq:TM], lhsT=kT[r0:r0 + 64, hp, jk * 128:(jk + 1) * 128],
                                                  rhs=U[r0:r0 + 64, q0 + hp * TM + cq: q0 + (hp + 1) * TM], start=True, stop=True),
                         reads=[("kT", hp), ("q", hp)], writes=[("ps", bs)])
                else:
                    hd, mc = a_, jk
                    P.op("pe", lambda e: e.matmul(ps[bs][:, :TM], lhsT=km[:, hd, mc * 128:(mc + 1) * 128],
                                                  rhs=U[:, qm0 + hd * TM: qm0 + (hd + 1) * TM], start=True, stop=True),
                         reads=[("km", hd), ("qm", hd)], writes=[("ps", bs)])

            def it_rest(n):
                kind, a_, hh, jk = its[n]
                bs = spool[n % 4]
                pb = n % 3
                p0 = pt[pb]
                if kind == "f":
                    hp, r0 = a_, 64 * hh
                    hd = 2 * hp + hh
                    gi = hp
                    cq = max(0, jk - jq0) * 128
                    first, lastk = (jk == 0), (jk == nk - 1)
                    P.op("act", lambda e: e.activation(out=U[:, p0 + cq: p0 + TM], in_=ps[bs][:, cq:TM], func=AF.Exp,
                                                       bias=biast[:, jk, hd:hd + 1], scale=0.125),
                         reads=[("ps", bs), ("bias", jk)], writes=[("pt", pb)])
                    if jk >= jq0:
                        jb = jk - jq0
                        P.op("pool", lambda e: e.tensor_tensor(out=U[:, p0 + jb * 128: p0 + (jb + 1) * 128], in0=U[:, p0 + jb * 128: p0 + (jb + 1) * 128],
                                                               in1=tri_bf[:, :], op=ALU.mult),
                             reads=[("pt", pb), "tri_bf"], writes=[("pt", pb)])
                    vl, vkey, ol, rows = Vc[:, jk * c.FW + hd * 64: jk * c.FW + (hd + 1) * 64], ("Vc", jk), ones_bf[:, 0:64], slice(r0, r0 + 64)
                    fin = lastk and hh == 1
                    dst0, dkey = at0 + hp * TM, ("at", hp)
                else:
                    hd, mc = a_, jk
                    gi = HP + hd
                    cq = 0
                    first, lastk = (mc == 0), (mc == MC - 1)
                    P.op("act", lambda e: e.activation(out=U[:, p0: p0 + TM], in_=ps[bs][:, :TM], func=AF.Exp, scale=128.0 ** -0.5),
                         reads=[("ps", bs)], writes=[("pt", pb)])
                    vl, vkey, ol, rows = vm[:, mc, hd * 128:(hd + 1) * 128], ("vm", mc), ones_bf[:, :], slice(0, 128)
                    fin = lastk
                    dst0, dkey = mo0 + hd * TM, ("mo", hd)
                bn, bd = (4, 5) if gi % 2 == 0 else (6, 7)
                P.op("pe", lambda e: e.matmul(ps[bn][rows, cq:TM], lhsT=vl, rhs=U[:, p0 + cq: p0 + TM], start=first, stop=lastk),
                     reads=[("pt", pb), vkey], writes=[("ps", bn)], signal=False)
                P.op("pe", lambda e: e.matmul(ps[bd][rows, cq:TM], lhsT=ol, rhs=U[:, p0 + cq: p0 + TM], start=first, stop=lastk),
                     reads=[("pt", pb), "ones_bf"], writes=[("ps", bd)], signal=True)
                if fin:
                    r_ = rd[gi % 2]
                    P.op("dve", lambda e: e.reciprocal(out=U32[:, r_:r_ + TM], in_=ps[bd][:, :TM]), reads=[("ps", bd)], writes=[("rd", gi % 2)])
                    P.op("dve", lambda e: e.tensor_tensor(out=U[:, dst0: dst0 + TM], in0=ps[bn][:, :TM], in1=U32[:, r_:r_ + TM], op=ALU.mult),
                         reads=[("ps", bn), ("rd", gi % 2)], writes=[dkey])

            LA = 2
            for n in range(len(its) + LA):
                if n < len(its):
                    it_qk(n)
                if n >= LA:
                    it_rest(n - LA)

            for dc in range(KC):
                wa, wak = wtile(j)
                wb, wbk = wtile(j + 1)
                j += 2
                o = KC * 128
                branches = (
                    ([wa[:, kc * 128:(kc + 1) * 128] for kc in range(KC)], wak,
                     [(wb[:, o + kk * 128: o + (kk + 1) * 128], U[:, mx0 + kk * TM: mx0 + (kk + 1) * TM]) for kk in range(4)], [("mx", g) for g in range(4)]),
                    ([wa[:, (KC + kc) * 128:(KC + kc + 1) * 128] for kc in range(KC)], wak,
                     [(wb[:, o + (4 + kk) * 128: o + (5 + kk) * 128], U[:, at0 + kk * TM: at0 + (kk + 1) * TM]) for kk in range(HP)], [("at", g) for g in range(HP)]),
                    ([wb[:, kc * 128:(kc + 1) * 128] for kc in range(KC)], wbk,
                     [(wb[:, o + (4 + HP + kk) * 128: o + (5 + HP + kk) * 128], U[:, mo0 + kk * TM: mo0 + (kk + 1) * TM]) for kk in range(HM)], [("mo", g) for g in range(HM)]),
                )
                for bi, (gws, gk, ypairs, ykeys) in enumerate(branches):
                    bg, by = bank(), bank()
                    mm_group(ps[bg][:, :TM], [(gws[kc], hc(kc)) for kc in range(KC)], hkeys + [gk], ("ps", bg))
                    mm_group(ps[by][:, :TM], ypairs, ykeys + [wbk], ("ps", by))
                    P.op("act", lambda e, bg=bg, bi=bi: e.activation(out=U32[:, sgt[bi]: sgt[bi] + TM], in_=ps[bg][:, :TM], func=AF.Sigmoid),
                         reads=[("ps", bg)], writes=[("sgt", bi)])
                    P.op("dve", lambda e, by=by, bi=bi: e.tensor_tensor(out=U32[:, tt[bi]: tt[bi] + TM], in0=ps[by][:, :TM], in1=U32[:, sgt[bi]: sgt[bi] + TM], op=ALU.mult),
                         reads=[("ps", by), ("sgt", bi)], writes=[("tt", bi)])
                P.op("pool", lambda e: e.tensor_tensor(out=U32[:, tt[0]: tt[0] + TM], in0=U32[:, tt[0]: tt[0] + TM], in1=U32[:, tt[1]: tt[1] + TM], op=ALU.add),
                     reads=[("tt", 0), ("tt", 1)], writes=[("tt", 0)])
                P.op("pool", lambda e, dc=dc: e.tensor_tensor(out=U[:, mg0 + dc * TM: mg0 + (dc + 1) * TM], in0=U32[:, tt[0]: tt[0] + TM], in1=U32[:, tt[2]: tt[2] + TM], op=ALU.add),
                     reads=[("tt", 0), ("tt", 2)], writes=[("mg", dc)])
            mgk = [("mg", dc) for dc in range(KC)]
            for t in range(KC // 2):
                wt, wk = wtile(j)
                j += 1
                for cc in range(2):
                    dc = 2 * t + cc
                    b = bank()
                    mm_group(ps[b][:, :TM], [(wt[:, kc * 256 + cc * 128: kc * 256 + (cc + 1) * 128], U[:, mg0 + kc * TM: mg0 + (kc + 1) * TM]) for kc in range(KC)],
                             mgk + [wk], ("ps", b))
                    P.op("dve", lambda e, b=b, dc=dc: e.tensor_tensor(out=x[:, dc, c0:c0 + TM], in0=ps[b][:, :TM], in1=x[:, dc, c0:c0 + TM], op=ALU.add),
                         reads=[("ps", b), ("x", dc)], writes=[("x", dc)])
            assert j == c.ffn_base[2]
            P.barrier()

        xkeys = [("x", kc) for kc in range(KC)]
        step = max(1, KC // 4)
        last = None
        for s in range(c.NSEQ):
            memkv(s)
            for i in range(S // T):
                toks = []
                for k0 in range(0, KC, step):
                    toks.append(P.op("act", lambda e, k0=k0, s=s, i=i: e.dma_start(
                        out=x[:, k0:k0 + step, :], in_=xd[s, k0:k0 + step, :, i * T:(i + 1) * T].rearrange("k p t -> p k t")),
                        writes=xkeys[k0:k0 + step], dma="xld"))
                P.retoken([], xkeys, toks, toks[-1])
                ffn(1, c.p_g1)
                rmsnorm_stream(lambda kc: x[:, kc, :], xkeys, lambda kc: h[:, kc, :], [("h", kc) for kc in range(KC)], c.p_gmix, KC, T, 1.0 / c.D)
                for c0 in range(0, T, TM):
                    mixer(s, i, c0)
                ffn(2, c.p_g2)
                toks = []
                for k0 in range(0, KC, step):
                    toks.append(P.op("act", lambda e, k0=k0, s=s, i=i: e.dma_start(
                        out=od[s, k0:k0 + step, :, i * T:(i + 1) * T].rearrange("k p t -> p k t"), in_=x[:, k0:k0 + step, :]),
                        reads=xkeys[k0:k0 + step], dma="xst"))
                P.retoken(xkeys, [], toks, toks[-1])
                last = toks[-1]
        flush_store()
        P.wait_all("act", [last])
        P.emit()
    return nc


_CACHE = {}


def kernel(**inputs):
    cfg = Cfg()
    inp = {k: np.asarray(v) for k, v in inputs.items()}
    ncores = 8
    B = inp["x"].shape[0]
    assert B == ncores * cfg.NSEQ
    wts, sw, prm = pack_weights(inp, cfg)
    x = inp["x"].astype(np.float32, copy=False)
    mem = inp["mem"].astype(np.float32, copy=False)
    in_maps = []
    for r in range(ncores):
        xs = x[r * cfg.NSEQ:(r + 1) * cfg.NSEQ]
        xT = np.ascontiguousarray(xs.transpose(0, 2, 1)).reshape(cfg.NSEQ, cfg.KC, 128, cfg.S)
        ms = mem[r * cfg.NSEQ:(r + 1) * cfg.NSEQ]
        mT = np.ascontiguousarray(ms.transpose(0, 2, 1)).reshape(cfg.NSEQ, cfg.KC, 128, cfg.M)
        in_maps.append({"xT": xT, "memT": mT, "wts": wts, "sw": sw, "prm": prm})
    if "nc" not in _CACHE:
        _CACHE["nc"] = build_program(cfg)
    res = run_bass_kernel_spmd(_CACHE["nc"], in_maps, core_ids=list(range(ncores)))
    out = np.empty((B, cfg.S, cfg.D), np.float32)
    for r in range(ncores):
        oT = np.asarray(res.results[r]["outT"]).reshape(cfg.NSEQ, cfg.D, cfg.S)
        out[r * cfg.NSEQ:(r + 1) * cfg.NSEQ] = oT.transpose(0, 2, 1)
    return out
```

```python
from contextlib import ExitStack
import numpy as np
import concourse.bass as bass
import concourse.mybir as mybir
from concourse.bass_utils import run_bass_kernel_spmd

F32 = mybir.dt.float32
BF16 = mybir.dt.bfloat16
ALU = mybir.AluOpType
AF = mybir.ActivationFunctionType
ENGS = ("pe", "act", "dve", "pool", "sp")
EPS = 1e-6
POOL_WINDOWS = (2, 4, 8, 16)


class Cfg:
    def __init__(self, D=2048, FF=5632, S=2048, NSEQ=2, HF=16, HM=4, M=256, T=512, TM=256, G=22, NS=4):
        self.D, self.FF, self.S, self.NSEQ, self.HF, self.HM, self.M = D, FF, S, NSEQ, HF, HM, M
        self.T, self.TM, self.G, self.NS = T, TM, G, NS
        self.KC = D // 128
        self.FC = FF // 128
        self.FW = HF * 64
        self.MW = HM * 128
        self.NPT = 2
        self.NQT = self.FW // 256
        self.NMT = self.MW // 256
        self.NGRP = self.FC // G
        self.n_ffn = self.FC + self.NGRP * self.KC
        self.n_mix = self.NPT + 3 * self.NQT + self.NMT + 2 * self.KC + self.KC // 2
        self.ffn_base = {1: 0, 2: self.n_ffn + self.n_mix}
        self.mix_base = self.n_ffn
        self.mk_base = 2 * self.n_ffn + self.n_mix
        self.NTILES = self.mk_base + 2 * self.NMT
        gb = self.KC * 128 + 4 * 128 + (HF // 2) * 128 + HM * 128
        self.LINE = max(2 * self.KC * 128, G * 128, self.KC * 256, gb)
        self.LINE += self.LINE % 2
        self.o_pool = 0
        self.o_q = 512
        self.o_k = 512 + self.FW
        self.o_v = 512 + 2 * self.FW
        self.o_f = 512 + 3 * self.FW
        self.o_qm = self.o_f + HF
        self.o_g = self.o_qm + self.MW
        c = 0
        self.p_g1 = c; c += self.KC
        self.p_gmix = c; c += self.KC
        self.p_g2 = c; c += self.KC
        self.p_gmem = c; c += self.KC
        self.p_psc = c; c += 4
        self.p_fq = c; c += 1
        self.p_fk = c; c += 1
        self.p_mq = c; c += 1
        self.p_mk = c; c += 1
        self.p_bf = c; c += HF
        self.p_rc = c; c += 64
        self.NPRM = c
        self.NSW = self.KC * HF + 4 * 128


def pack_weights(inp, cfg):
    KC, FC, G, D, FF = cfg.KC, cfg.FC, cfg.G, cfg.D, cfg.FF
    LINE = cfg.LINE
    wts = np.zeros((cfg.NTILES, 128, LINE), np.float32)

    def ffn_tiles(wgu, wd, base):
        gu = wgu.reshape(KC, 128, 2, FC, 128).transpose(3, 1, 2, 0, 4).reshape(FC, 128, 2 * KC * 128)
        dn = wd.reshape(cfg.NGRP, G, 128, KC, 128).transpose(0, 3, 2, 1, 4).reshape(cfg.NGRP, KC, 128, G * 128)
        j = base
        for grp in range(cfg.NGRP):
            wts[j:j + G, :, :2 * KC * 128] = gu[grp * G:(grp + 1) * G]
            j += G
            wts[j:j + KC, :, :G * 128] = dn[grp]
            j += KC

    ffn_tiles(inp["ffn1_w_gate_up"][0], inp["ffn1_w_down"][0], cfg.ffn_base[1])
    ffn_tiles(inp["ffn2_w_gate_up"][0], inp["ffn2_w_down"][0], cfg.ffn_base[2])
    win = inp["w_in"][0]

    def in_tile(c0):
        return win[:, c0:c0 + 256].reshape(KC, 128, 256).transpose(1, 0, 2).reshape(128, KC * 256)

    j = cfg.mix_base
    for off, n in ((cfg.o_pool, cfg.NPT), (cfg.o_q, cfg.NQT), (cfg.o_k, cfg.NQT), (cfg.o_v, cfg.NQT), (cfg.o_qm, cfg.NMT)):
        for t in range(n):
            wts[j, :, :KC * 256] = in_tile(off + 256 * t)
            j += 1
    wpu, wfo, wmo = inp["w_pool_up"][0], inp["w_fox_o"][0], inp["w_mem_o"][0]
    for dc in range(KC):
        ga = np.stack([win[:, cfg.o_g + b * D + dc * 128: cfg.o_g + b * D + (dc + 1) * 128] for b in (0, 1)])
        wts[j, :, :2 * KC * 128] = ga.reshape(2, KC, 128, 128).transpose(2, 0, 1, 3).reshape(128, -1)
        j += 1
        gm = win[:, cfg.o_g + 2 * D + dc * 128: cfg.o_g + 2 * D + (dc + 1) * 128]
        parts = [gm.reshape(KC, 128, 128).transpose(1, 0, 2).reshape(128, -1)]
        for w in (wpu, wfo, wmo):
            kk = w.shape[0] // 128
            parts.append(w[:, dc * 128:(dc + 1) * 128].reshape(kk, 128, 128).transpose(1, 0, 2).reshape(128, -1))
        gb = np.concatenate(parts, axis=1)
        wts[j, :, :gb.shape[1]] = gb
        j += 1
    wout = inp["w_out"][0]
    for t in range(KC // 2):
        wts[j, :, :KC * 256] = wout[:, t * 256:(t + 1) * 256].reshape(KC, 128, 256).transpose(1, 0, 2).reshape(128, -1)
        j += 1
    assert j == cfg.ffn_base[2]
    wkv = inp["w_mem_kv"][0]
    j = cfg.mk_base
    for t in range(2 * cfg.NMT):
        wts[j, :, :KC * 256] = wkv[:, t * 256:(t + 1) * 256].reshape(KC, 128, 256).transpose(1, 0, 2).reshape(128, -1)
        j += 1
    sw = np.zeros((128, cfg.NSW), np.float32)
    sw[:, :KC * cfg.HF] = win[:, cfg.o_f:cfg.o_f + cfg.HF].reshape(KC, 128, cfg.HF).transpose(1, 0, 2).reshape(128, -1)
    sw[:, KC * cfg.HF:] = inp["pool_w"][0].transpose(1, 0, 2).reshape(128, 4 * 128)
    prm = np.zeros((128, cfg.NPRM), np.float32)
    col = lambda v: np.asarray(v, np.float32).reshape(-1, 128).T
    prm[:, cfg.p_g1:cfg.p_g1 + KC] = col(inp["ffn1_norm"][0])
    prm[:, cfg.p_gmix:cfg.p_gmix + KC] = col(inp["mix_norm"][0])
    prm[:, cfg.p_g2:cfg.p_g2 + KC] = col(inp["ffn2_norm"][0])
    prm[:, cfg.p_gmem:cfg.p_gmem + KC] = col(inp["mem_norm"][0])
    prm[:, cfg.p_psc:cfg.p_psc + 4] = col(inp["pool_scale"][0])
    prm[:, cfg.p_fq] = np.tile(np.asarray(inp["fox_q_norm"][0], np.float32), 2)
    prm[:, cfg.p_fk] = np.tile(np.asarray(inp["fox_k_norm"][0], np.float32), 2)
    prm[:, cfg.p_mq] = np.asarray(inp["mem_q_norm"][0], np.float32)
    prm[:, cfg.p_mk] = np.asarray(inp["mem_k_norm"][0], np.float32)
    prm[:, cfg.p_bf:cfg.p_bf + cfg.HF] = np.asarray(inp["b_forget"][0], np.float32)[None, :]
    for g, w in enumerate(POOL_WINDOWS):
        prm[:, cfg.p_rc + 16 * g: cfg.p_rc + 16 * g + 16] = (1.0 / np.minimum(np.arange(1, 17), w))[None, :]
    return wts, sw, prm


class Prog:
    def __init__(self, nc, stack):
        self.nc, self.stack = nc, stack
        self.ops = {e: [] for e in ENGS}
        self.count = {e: 0 for e in ENGS}
        self.seen = {e: {} for e in ENGS}
        self.sem = {}
        self.dcount = {}
        self.buf = {}
        self.alias = {}

    def _expand(self, keys):
        out = []
        for k in keys:
            out.append(k)
            out.extend(self.alias.get(k, ()))
        return out

    def _sem(self, src):
        if src not in self.sem:
            self.sem[src] = self.stack.enter_context(self.nc.semaphore("s_" + src.replace(":", "_")))
        return self.sem[src]

    def op(self, eng, fn, reads=(), writes=(), signal=True, dma=None):
        reads, writes = self._expand(reads), self._expand(writes)
        deps = {}

        def add(tok):
            if tok is not None and deps.get(tok[0], 0) < tok[1]:
                deps[tok[0]] = tok[1]

        for k in reads:
            st = self.buf.get(k)
            if st is not None:
                add(st[0])
        for k in writes:
            st = self.buf.get(k)
            if st is not None:
                add(st[0])
                for t in st[1]:
                    add(t)
        waits = []
        seen = self.seen[eng]
        for s, v in deps.items():
            if seen.get(s, 0) >= v or (s == "pe" and eng == "pe"):
                continue
            seen[s] = v
            waits.append((s, v))
        if dma is not None:
            src = "dma:" + dma
            self.dcount[src] = self.dcount.get(src, 0) + 16
            tok, inc = (src, self.dcount[src]), (src, 16)
        elif signal:
            self.count[eng] += 1
            tok, inc = (eng, self.count[eng]), (eng, 1)
        else:
            tok, inc = (eng, self.count[eng] + 1), None
        self.ops[eng].append((waits, fn, inc))
        for k in reads:
            self.buf.setdefault(k, [None, []])[1].append(tok)
        for k in writes:
            self.buf[k] = [tok, []]
        return tok

    def retoken(self, keys_r, keys_w, old_toks, tok):
        olds = set(old_toks)
        for k in keys_r:
            st = self.buf[k]
            st[1] = [t for t in st[1] if t not in olds] + [tok]
        for k in keys_w:
            self.buf[k] = [tok, []]

    def barrier(self, extra=()):
        for e in ("pe", "act", "dve", "pool"):
            waits = []
            for s in ("pe", "act", "dve", "pool"):
                v = self.count[s]
                if s != e and v > self.seen[e].get(s, 0):
                    self.seen[e][s] = v
                    waits.append((s, v))
            for s, v in extra:
                if v > self.seen[e].get(s, 0):
                    self.seen[e][s] = v
                    waits.append((s, v))
            self.ops[e].append((waits, None, None))

    def wait_all(self, eng, toks):
        self.ops[eng].append((list(toks), None, None))

    def emit(self):
        nc, ops, sem = self.nc, self.ops, self._sem
        for s_ in list(self.dcount) + ["pe", "act", "dve", "pool"]:
            sem(s_)

        def run(engine, lst):
            for waits, fn, inc in lst:
                for s, v in waits:
                    engine.wait_ge(sem(s), v)
                if fn is None:
                    continue
                ins = fn(engine)
                if inc is not None:
                    ins.then_inc(sem(inc[0]), inc[1])

        with nc.Block() as block:
            @block.tensor
            def _(e):
                run(e, ops["pe"])

            @block.scalar
            def _(e):
                run(e, ops["act"])

            @block.vector
            def _(e):
                run(e, ops["dve"])

            @block.gpsimd
            def _(e):
                run(e, ops["pool"])

            @block.sync
            def _(e):
                run(e, ops["sp"])


def build_program(cfg, prepass=True, background=True):
    c = cfg
    KC, FC, G, T, TM, HF, HM, M, S, NS, LINE = c.KC, c.FC, c.G, c.T, c.TM, c.HF, c.HM, c.M, c.S, c.NS, c.LINE
    NJ = S // 128
    HP = HF // 2
    MC = M // 128
    H2 = LINE // 2
    nc = bass.Bass("TRN2", target_bir_lowering=False)
    xd = nc.dram_tensor("xT", [c.NSEQ, KC, 128, S], F32, kind="ExternalInput").ap()
    md = nc.dram_tensor("memT", [c.NSEQ, KC, 128, M], F32, kind="ExternalInput").ap()
    wts = nc.dram_tensor("wts", [c.NTILES, 128, LINE], F32, kind="ExternalInput").ap()
    swd = nc.dram_tensor("sw", [128, c.NSW], F32, kind="ExternalInput").ap()
    prd = nc.dram_tensor("prm", [128, c.NPRM], F32, kind="ExternalInput").ap()
    od = nc.dram_tensor("outT", [c.NSEQ, KC, 128, S], F32, kind="ExternalOutput").ap()
    wtb = nc.dram_tensor("wtb", [c.NTILES, 128, LINE], BF16).ap()

    with ExitStack() as st:
        P = Prog(nc, st)
        sb = lambda name, shape, dt: st.enter_context(nc.sbuf_tensor(name, shape, dt))
        x = sb("x", [128, KC, T], F32)
        h = sb("h", [128, KC, T], BF16)
        kT = sb("kT", [128, HP, S], BF16)
        Vc = sb("Vc", [128, NJ * c.FW], BF16)
        Vc32 = Vc.bitcast(F32)
        km = sb("km", [128, HM, M], BF16)
        vm = sb("vm", [128, MC, c.MW], BF16)
        ring = [sb("ring%d" % i, [128, LINE], BF16) for i in range(NS)]
        sq = [sb("sq%d" % i, [128, T], BF16) for i in range(2)]
        sd = sb("sd", [128, T], F32)
        rstd = sb("rstd", [128, T], F32)
        NQN = max(TM, M)
        sdq = [sb("sdq%d" % i, [128, NQN], F32) for i in range(2)]
        rstdq = [sb("rstdq%d" % i, [128, NQN], F32) for i in range(2)]
        ones_bf = sb("ones_bf", [128, 128], BF16)
        blk64 = sb("blk64", [128, 128], BF16)
        tri = sb("tri", [128, 128], F32)
        tri_bf = sb("tri_bf", [128, 128], BF16)
        ones_f = sb("ones_f", [128, 128], F32)
        prm = sb("prm_sb", [128, c.NPRM], F32)
        swb = sb("swb", [128, c.NSW], BF16)
        Cneg = sb("Cneg", [128, NJ, HF], F32)
        carry = sb("carry", [128, NJ + 1, HF], F32)
        NQ = TM // 128
        biast = sb("biast", [128, NJ, HF], F32)
        zt = sb("zt", [128, 3, HF], F32)
        uhist = sb("uhist", [128, 4, 16], F32)
        ffn_b = G * T * 2 + 2 * T * 4
        W16 = 16 + TM
        mix_b = (4 * W16 * 4 + 2 * W16 * 4 + 4 * TM * 2 + 4 * TM * 2 + HP * TM * 2 + HM * TM * 2 + HP * TM * 2
                 + HM * TM * 2 + 3 * TM * 2 + 2 * TM * 4 + 3 * TM * 4 + 3 * TM * 4 + KC * TM * 2 + 64 * 4)
        mem_b = KC * M * 4 + KC * M * 2
        pre_b = 3 * H2 * 4 + 3 * H2 * 2
        UB = max(ffn_b, mix_b, mem_b, pre_b)
        UB += (-UB) % 64
        U = sb("U", [128, UB // 2], BF16)
        U32 = U.bitcast(F32)

        class Carve:
            def __init__(self):
                self.off = 0

            def bf(self, n):
                assert self.off % 4 == 0
                a = self.off // 2
                self.off += n * 2
                self.off += (-self.off) % 4
                assert self.off <= UB
                return a, a + n

            def f32(self, n):
                a = self.off // 4
                self.off += n * 4
                assert self.off <= UB
                return a, a + n

        ps = [st.enter_context(nc.psum_tensor("ps%d" % i, [128, 512], F32)) for i in range(8)]
        rr = {"i": 0}

        def bank(pool=(0, 1, 2, 3, 4, 5, 6, 7)):
            rr["i"] += 1
            return pool[rr["i"] % len(pool)]

        pcol = lambda cidx: prm[:, cidx:cidx + 1]

        P.op("sp", lambda e: e.dma_start(out=prm[:, :], in_=prd), writes=["prm"], dma="prm")
        P.op("sp", lambda e: e.dma_start(out=U32[:, 0:c.NSW], in_=swd), writes=[("sbb", 0)], dma="swf")
        P.op("dve", lambda e: e.memset(ones_bf[:, :], 1.0), writes=["ones_bf"])
        P.op("dve", lambda e: e.memset(ones_f[:, :], 1.0), writes=["ones_f"])
        P.op("dve", lambda e: e.memset(blk64[:, :], 0.0), writes=["blk64"])
        P.op("dve", lambda e: e.memset(blk64[0:64, 0:64], 1.0), writes=["blk64"])
        P.op("dve", lambda e: e.memset(blk64[64:128, 64:128], 1.0), writes=["blk64"])
        P.op("pool", lambda e: e.affine_select(out=tri[:, :], in_=ones_f[:, :], pattern=[[1, 128]], compare_op=ALU.is_ge,
                                               fill=0.0, base=0, channel_multiplier=-1), reads=["ones_f"], writes=["tri"])
        P.op("dve", lambda e: e.tensor_copy(out=tri_bf[:, :], in_=tri[:, :]), reads=["tri"], writes=["tri_bf"])
        P.op("dve", lambda e: e.tensor_copy(out=swb[:, :], in_=U32[:, 0:c.NSW]), reads=[("sbb", 0)], writes=["swb"])
        P.barrier()
        wf_ap = lambda kc: swb[:, kc * HF:(kc + 1) * HF]
        poolw_ap = lambda g: swb[:, KC * HF + g * 128: KC * HF + (g + 1) * 128]

        bg_mode = prepass and background
        bg_tiles = []
        if prepass:
            cv = Carve()
            NSF, NSB, PLA = 4, 4, 3
            assert NSF * H2 <= NJ * c.FW // 2 or True
            if NSF * H2 <= NJ * c.FW // 2:
                sf_ap = lambda b_: Vc32[:, b_ * H2:(b_ + 1) * H2]
            else:
                sft = sb("sft", [128, NSF * H2], F32)
                sf_ap = lambda b_: sft[:, b_ * H2:(b_ + 1) * H2]
            sbb = [cv.bf(H2) for _ in range(NSB)]
            ceng = ("dve", "act")
            bg_tiles = list(range(c.ffn_base[2], c.ffn_base[2] + c.n_ffn)) if bg_mode else []
            pre_tiles = [j for j in range(c.NTILES) if j not in set(bg_tiles)]
            chunks = [(j, hf) for j in pre_tiles for hf in range(2)]
            NCH = len(chunks)

            def pre_load(n):
                j, hf = chunks[n]
                b = n % NSF
                src = wts[j, :, hf * H2:(hf + 1) * H2]
                P.op("sp", lambda e: e.dma_start(out=sf_ap(b), in_=src), writes=[("sf", b)], dma="sf%d" % b)

            def pre_cast_store(n):
                j, hf = chunks[n]
                b = n % NSF
                b2 = n % NSB
                ba, bb = sbb[b2]
                dst = wtb[j, :, hf * H2:(hf + 1) * H2]
                ce = ceng[n % 2]
                if ce == "act":
                    P.op("act", lambda e: e.activation(out=U[:, ba:bb], in_=sf_ap(b), func=AF.Copy), reads=[("sf", b)], writes=[("sbb", b2)])
                else:
                    P.op(ce, lambda e: e.tensor_copy(out=U[:, ba:bb], in_=sf_ap(b)), reads=[("sf", b)], writes=[("sbb", b2)])
                P.op("sp", lambda e: e.dma_start(out=dst, in_=U[:, ba:bb]), reads=[("sbb", b2)], dma="sb%d" % b2)

            for n in range(NCH + PLA):
                if n < NCH:
                    pre_load(n)
                if n >= PLA:
                    pre_cast_store(n - PLA)
            extra = [(s_, v) for s_, v in P.dcount.items() if s_.startswith("dma:s")]
            P.wait_all("sp", extra)
            P.barrier(extra)

        wn = {"n": 0}

        fresh_mode = not prepass
        done = set()
        pending = {"s": None}
        stn = {"n": 0}
        PL = LINE // 4
        base32 = (T // 128) * c.FW // 2
        NSTG = 4
        if fresh_mode:
            if (NJ * c.FW // 2 - base32) // PL >= 4:
                NSTG = (NJ * c.FW // 2 - base32) // PL
                stage_ap = lambda b_: Vc32[:, base32 + b_ * PL: base32 + (b_ + 1) * PL]
            else:
                stg_t = sb("stg_t", [128, 4 * PL], F32)
                stage_ap = lambda b_: stg_t[:, b_ * PL:(b_ + 1) * PL]
        cast_eng = ("dve", "act", "pool", "pool")
        for sl_ in range(NS):
            P.alias[("ring", sl_)] = [("ringq", sl_, q_) for q_ in range(4)]

        bgst = {"n": 0, "pend": None}
        bg_chunks = [(j, hf) for j in bg_tiles for hf in range(2)]
        bg_set = set(bg_tiles)
        if bg_mode:
            for j_ in bg_tiles:
                P.alias[("wtb", j_)] = [("wtbh", j_, 0), ("wtbh", j_, 1)]
            vc_used = (T // 128) * c.FW
            if NJ * c.FW - vc_used >= 6 * H2 and vc_used <= 2 * H2:
                bgf_ap = lambda b_: Vc32[:, 2 * H2 + b_ * H2: 2 * H2 + (b_ + 1) * H2]
                bgb_ap = lambda b_: Vc[:, 2 * H2 + b_ * H2: 2 * H2 + (b_ + 1) * H2]
            else:
                bgf_t = sb("bgf_t", [128, 2 * H2], F32)
                bgb_t = sb("bgb_t", [128, 2 * H2], BF16)
                bgf_ap = lambda b_: bgf_t[:, b_ * H2:(b_ + 1) * H2]
                bgb_ap = lambda b_: bgb_t[:, b_ * H2:(b_ + 1) * H2]

        def bg_store():
            if bgst["pend"] is not None:
                j_, hf_, b_ = bgst["pend"]
                bgst["pend"] = None
                P.op("sp", lambda e: e.dma_start(out=wtb[j_, :, hf_ * H2:(hf_ + 1) * H2], in_=bgb_ap(b_)),
                     reads=[("bgb", b_)], writes=[("wtbh", j_, hf_)], dma="bgb%d" % b_)

        def bg_step():
            k = bgst["n"]
            if k >= len(bg_chunks):
                bg_store()
                return False
            bgst["n"] += 1
            j_, hf_ = bg_chunks[k]
            b_ = k % 2
            P.op("sp", lambda e: e.dma_start(out=bgf_ap(b_), in_=wts[j_, :, hf_ * H2:(hf_ + 1) * H2]), writes=[("bgf", b_)], dma="bgf%d" % b_)
            bg_store()
            P.op("pool", lambda e: e.tensor_copy(out=bgb_ap(b_), in_=bgf_ap(b_)), reads=[("bgf", b_)], writes=[("bgb", b_)])
            bgst["pend"] = (j_, hf_, b_)
            return True

        def flush_store():
            if pending["s"] is not None:
                j_, slot_ = pending["s"]
                pending["s"] = None
                P.op("sp", lambda e: e.dma_start(out=wtb[j_, :, :], in_=ring[slot_][:, :]), reads=[("ring", slot_)], writes=[("wtb", j_)], dma="ws%d" % slot_)

        def wtile(j):
            slot = wn["n"] % NS
            wn["n"] += 1
            if fresh_mode and j not in done:
                done.add(j)
                for q in range(4):
                    b_ = stn["n"] % NSTG
                    stn["n"] += 1
                    sa = stage_ap(b_)
                    src = wts[j, :, q * PL:(q + 1) * PL]
                    P.op("sp", lambda e, sa=sa, src=src: e.dma_start(out=sa, in_=src), writes=[("stg", b_)], dma="stg%d" % b_)
                    ce = cast_eng[q]
                    if ce == "act":
                        P.op("act", lambda e, sa=sa, q=q: e.activation(out=ring[slot][:, q * PL:(q + 1) * PL], in_=sa, func=AF.Copy),
                             reads=[("stg", b_)], writes=[("ringq", slot, q)])
                    else:
                        P.op(ce, lambda e, sa=sa, q=q: e.tensor_copy(out=ring[slot][:, q * PL:(q + 1) * PL], in_=sa),
                             reads=[("stg", b_)], writes=[("ringq", slot, q)])
                flush_store()
                pending["s"] = (j, slot)
            else:
                flush_store()
                if bg_mode and j in bg_set:
                    while bg_step():
                        pass
                P.op("sp", lambda e: e.dma_start(out=ring[slot][:, :], in_=wtb[j, :, :]), reads=[("wtb", j)], writes=[("ring", slot)], dma="w%d" % slot)
                if bg_mode:
                    bg_step()
            return ring[slot], ("ring", slot)

        def mm_group(out_ap, pairs, reads, bankkey, start=True, stop=True, each=None):
            n = len(pairs)
            for i_, (l, r) in enumerate(pairs):
                rd_ = reads if each is None else list(reads) + [each[i_]]
                P.op("pe", lambda e, l=l, r=r, i_=i_: e.matmul(out_ap, lhsT=l, rhs=r, start=(start and i_ == 0), stop=(stop and i_ == n - 1)),
                     reads=rd_, writes=[bankkey], signal=(i_ == n - 1))

        def norm_finish(ssq_bank, n, inv_n):
            P.op("act", lambda e: e.activation(out=sd[:, :n], in_=ps[ssq_bank][:, :n], func=AF.Sqrt, bias=EPS, scale=inv_n),
                 reads=[("ps", ssq_bank)], writes=["sd"])
            P.op("dve", lambda e: e.reciprocal(out=rstd[:, :n], in_=sd[:, :n]), reads=["sd"], writes=["rstd"])

        def rmsnorm_stream(src_fn, src_keys, dst_fn, dst_keys, gcol0, nk, n, inv_n):
            bs = bank()
            for kc in range(nk):
                b = kc % 2
                P.op("act", lambda e, kc=kc, b=b: e.activation(out=sq[b][:, :n], in_=src_fn(kc), func=AF.Square),
                     reads=[src_keys[kc]], writes=[("sq", b)])
                P.op("pe", lambda e, kc=kc, b=b: e.matmul(ps[bs][:, :n], lhsT=ones_bf[:, :], rhs=sq[b][:, :n], start=(kc == 0), stop=(kc == nk - 1)),
                     reads=[("sq", b), "ones_bf"], writes=[("ps", bs)], signal=True)
            norm_finish(bs, n, inv_n)
            for kc in range(nk):
                P.op("dve", lambda e, kc=kc: e.scalar_tensor_tensor(out=dst_fn(kc), in0=src_fn(kc), scalar=pcol(gcol0 + kc), in1=rstd[:, :n],
                                                                  op0=ALU.mult, op1=ALU.mult),
                     reads=[src_keys[kc], "rstd", "prm"], writes=[dst_keys[kc]])

        def qknorm_a(src_bank, n, k):
            P.op("act", lambda e: e.activation(out=sq[k][:, :n], in_=ps[src_bank][:, :n], func=AF.Square),
                 reads=[("ps", src_bank)], writes=[("sq", k)])

        def qknorm_b(src_bank, n, k, ones_ap, ones_key, inv_n, gcol, dst_ap, dst_key):
            b = bank()
            P.op("pe", lambda e: e.matmul(ps[b][:, :n], lhsT=ones_ap, rhs=sq[k][:, :n], start=True, stop=True),
                 reads=[("sq", k), ones_key], writes=[("ps", b)])
            P.op("act", lambda e: e.activation(out=sdq[k][:, :n], in_=ps[b][:, :n], func=AF.Sqrt, bias=EPS, scale=inv_n),
                 reads=[("ps", b)], writes=[("sdq", k)])
            P.op("dve", lambda e: e.reciprocal(out=rstdq[k][:, :n], in_=sdq[k][:, :n]), reads=[("sdq", k)], writes=[("rstdq", k)])
            P.op("dve", lambda e: e.scalar_tensor_tensor(out=dst_ap, in0=ps[src_bank][:, :n], scalar=pcol(gcol), in1=rstdq[k][:, :n],
                                                         op0=ALU.mult, op1=ALU.mult),
                 reads=[("ps", src_bank), ("rstdq", k), "prm"], writes=[dst_key])

        def qknorm(src_bank, n, ones_ap, ones_key, inv_n, gcol, dst_ap, dst_key):
            qknorm_a(src_bank, n, 0)
            qknorm_b(src_bank, n, 0, ones_ap, ones_key, inv_n, gcol, dst_ap, dst_key)

        def ffn(idx, gcol0):
            cv = Carve()
            act_a, _ = cv.bf(G * T)
            sg = [cv.f32(T) for _ in range(2)]
            rmsnorm_stream(lambda kc: x[:, kc, :], [("x", kc) for kc in range(KC)],
                           lambda kc: h[:, kc, :], [("h", kc) for kc in range(KC)], gcol0, KC, T, 1.0 / c.D)
            hkeys = [("h", kc) for kc in range(KC)]
            j = c.ffn_base[idx]
            for grp in range(c.NGRP):
                for ff in range(G):
                    wt, wk = wtile(j)
                    j += 1
                    bg, bu = bank(), bank()
                    mm_group(ps[bg][:, :T], [(wt[:, kc * 128:(kc + 1) * 128], h[:, kc, :]) for kc in range(KC)], [wk], ("ps", bg), each=hkeys)
                    mm_group(ps[bu][:, :T], [(wt[:, (KC + kc) * 128:(KC + kc + 1) * 128], h[:, kc, :]) for kc in range(KC)], [wk], ("ps", bu), each=hkeys)
                    s0, s1 = sg[ff % 2]
                    P.op("act", lambda e, bg=bg, s0=s0, s1=s1: e.activation(out=U32[:, s0:s1], in_=ps[bg][:, :T], func=AF.Silu),
                         reads=[("ps", bg)], writes=[("sg", ff % 2)])
                    a0 = act_a + ff * T
                    P.op("dve", lambda e, bu=bu, s0=s0, s1=s1, a0=a0: e.tensor_tensor(out=U[:, a0:a0 + T], in0=ps[bu][:, :T], in1=U32[:, s0:s1], op=ALU.mult),
                         reads=[("ps", bu), ("sg", ff % 2)], writes=[("act", ff)])
                akeys = [("act", ff) for ff in range(G)]
                for dc in range(KC):
                    wt, wk = wtile(j)
                    j += 1
                    bd = bank()
                    mm_group(ps[bd][:, :T], [(wt[:, kk * 128:(kk + 1) * 128], U[:, act_a + kk * T: act_a + (kk + 1) * T]) for kk in range(G)],
                             [wk], ("ps", bd), each=akeys)
                    P.op("dve", lambda e, bd=bd, dc=dc: e.scalar_tensor_tensor(out=x[:, dc, :], in0=ps[bd][:, :T], scalar=0.5, in1=x[:, dc, :],
                                                                             op0=ALU.mult, op1=ALU.add),
                         reads=[("ps", bd), ("x", dc)], writes=[("x", dc)])
            P.barrier()

        def memkv(s):
            cv = Carve()
            mf0, _ = cv.f32(KC * M)
            hm0, _ = cv.bf(KC * M)
            toks = []
            step = max(1, KC // 4)
            for k0 in range(0, KC, step):
                toks.append(P.op("act", lambda e, k0=k0: e.dma_start(
                    out=U32[:, mf0 + k0 * M: mf0 + (k0 + step) * M].rearrange("p (k m) -> p k m", k=step),
                    in_=md[s, k0:k0 + step, :, :].rearrange("k p m -> p k m")), writes=["memf"], dma="mem"))
            P.retoken([], ["memf"], toks, toks[-1])
            mfk = ["memf"] * KC
            rmsnorm_stream(lambda kc: U32[:, mf0 + kc * M: mf0 + (kc + 1) * M], mfk,
                           lambda kc: U[:, hm0 + kc * M: hm0 + (kc + 1) * M], [("hm", kc) for kc in range(KC)], c.p_gmem, KC, M, 1.0 / c.D)
            hmk = [("hm", kc) for kc in range(KC)]
            hm_ap = lambda kc, a, b: U[:, hm0 + kc * M + a: hm0 + kc * M + b]
            for t in range(c.NMT):
                wt, wk = wtile(c.mk_base + t)
                for cc in range(2):
                    hd = 2 * t + cc
                    b = bank()
                    mm_group(ps[b][:, :M], [(wt[:, kc * 256 + cc * 128: kc * 256 + (cc + 1) * 128], hm_ap(kc, 0, M)) for kc in range(KC)],
                             hmk + [wk], ("ps", b))
                    qknorm(b, M, ones_bf[:, :], "ones_bf", 1.0 / 128, c.p_mk, km[:, hd, :], ("km", hd))
            for t in range(c.NMT):
                wt, wk = wtile(c.mk_base + c.NMT + t)
                for mc in range(MC):
                    b = bank()
                    mm_group(ps[b][:, :256], [(hm_ap(kc, mc * 128, (mc + 1) * 128), wt[:, kc * 256:(kc + 1) * 256]) for kc in range(KC)],
                             hmk + [wk], ("ps", b))
                    P.op("act", lambda e, b=b, mc=mc, t=t: e.activation(out=vm[:, mc, t * 256:(t + 1) * 256], in_=ps[b][:, :256], func=AF.Copy),
                         reads=[("ps", b)], writes=[("vm", mc)])
            P.barrier()

        def mixer(s, i, c0):
            t0 = i * T + c0
            jq0 = t0 // 128
            cv = Carve()
            ub0, _ = cv.f32(4 * W16)
            lv = [cv.f32(W16)[0] for _ in range(2)]
            t16, _ = cv.f32(64)
            db0, _ = cv.bf(4 * TM)
            mx0, _ = cv.bf(4 * TM)
            q0, _ = cv.bf(HP * TM)
            qm0, _ = cv.bf(HM * TM)
            at0, _ = cv.bf(HP * TM)
            mo0, _ = cv.bf(HM * TM)
            pt = [cv.bf(TM)[0] for _ in range(3)]
            rd = [cv.f32(TM)[0] for _ in range(2)]
            sgt = [cv.f32(TM)[0] for _ in range(3)]
            tt = [cv.f32(TM)[0] for _ in range(3)]
            mg0, _ = cv.bf(KC * TM)
            hc = lambda kc: h[:, kc, c0:c0 + TM]
            hkeys = [("h", kc) for kc in range(KC)]
            ub = lambda g, a, b: U32[:, ub0 + g * W16 + a: ub0 + g * W16 + b]
            j = c.mix_base

            for g in range(4):
                if t0 == 0:
                    P.op("pool", lambda e, g=g: e.memset(ub(g, 0, 16), 0.0), writes=[("ub", g)])
                else:
                    P.op("pool", lambda e, g=g: e.tensor_copy(out=ub(g, 0, 16), in_=uhist[:, g, :]), reads=[("uhist", g)], writes=[("ub", g)])
            for t in range(c.NPT):
                wt, wk = wtile(j)
                j += 1
                for cc in range(2):
                    g = 2 * t + cc
                    b = bank()
                    mm_group(ps[b][:, :TM], [(wt[:, kc * 256 + cc * 128: kc * 256 + (cc + 1) * 128], hc(kc)) for kc in range(KC)], hkeys + [wk], ("ps", b))
                    P.op("act", lambda e, b=b, g=g: e.activation(out=ub(g, 16, W16), in_=ps[b][:, :TM], func=AF.Copy),
                         reads=[("ps", b)], writes=[("ub", g)])
            for g in range(4):
                w = POOL_WINDOWS[g]
                cur = lambda a, b, g=g: ub(g, a, b)
                curk = ("ub", g)
                for lvl in range(g + 1):
                    sh = 1 << lvl
                    dst0 = lv[lvl % 2]
                    dst = lambda a, b, dst0=dst0: U32[:, dst0 + a: dst0 + b]
                    P.op("pool", lambda e, cur=cur, dst=dst, sh=sh: e.tensor_tensor(out=dst(sh, W16), in0=cur(sh, W16), in1=cur(0, W16 - sh), op=ALU.add),
                         reads=[curk], writes=[("lv", lvl % 2)])
                    cur, curk = dst, ("lv", lvl % 2)
                P.op("dve", lambda e, cur=cur, g=g, w=w: e.scalar_tensor_tensor(out=U[:, db0 + g * TM: db0 + (g + 1) * TM], in0=cur(16, W16), scalar=1.0 / w,
                                                                              in1=ub(g, 16, W16), op0=ALU.mult, op1=ALU.subtract),
                     reads=[curk, ("ub", g)], writes=[("db", g)])
                if t0 == 0:
                    P.op("dve", lambda e, cur=cur, g=g: e.tensor_tensor(out=U32[:, t16 + 16 * g: t16 + 16 * g + 16], in0=cur(16, 32),
                                                                      in1=prm[:, c.p_rc + 16 * g: c.p_rc + 16 * g + 16], op=ALU.mult),
                         reads=[curk, "prm"], writes=[("t16", g)])
                    P.op("dve", lambda e, g=g: e.tensor_tensor(out=U[:, db0 + g * TM: db0 + g * TM + 16], in0=U32[:, t16 + 16 * g: t16 + 16 * g + 16],
                                                             in1=ub(g, 16, 32), op=ALU.subtract),
                         reads=[("t16", g), ("ub", g)], writes=[("db", g)])
                b = bank()
                P.op("pe", lambda e, b=b, g=g: e.matmul(ps[b][:, :TM], lhsT=poolw_ap(g), rhs=U[:, db0 + g * TM: db0 + (g + 1) * TM], start=True, stop=True),
                     reads=[("db", g), "swb"], writes=[("ps", b)])
                P.op("dve", lambda e, b=b, g=g: e.tensor_scalar(out=U[:, mx0 + g * TM: mx0 + (g + 1) * TM], in0=ps[b][:, :TM], scalar1=pcol(c.p_psc + g),
                                                              scalar2=None, op0=ALU.mult),
                     reads=[("ps", b), "prm"], writes=[("mx", g)])
                P.op("pool", lambda e, g=g: e.tensor_copy(out=uhist[:, g, :], in_=ub(g, TM, TM + 16)), reads=[("ub", g)], writes=[("uhist", g)])

            jb_q = c.mix_base + c.NPT
            jobs = []
            for t in range(c.NQT):
                for cc in range(2):
                    hp = 2 * t + cc
                    jobs.append((jb_q + t, cc, blk64[:, :], "blk64", 1.0 / 64, c.p_fq, U[:, q0 + hp * TM: q0 + (hp + 1) * TM], ("q", hp)))
            for t in range(c.NQT):
                for cc in range(2):
                    hp = 2 * t + cc
                    jobs.append((jb_q + c.NQT + t, cc, blk64[:, :], "blk64", 1.0 / 64, c.p_fk, kT[:, hp, t0:t0 + TM], ("kT", hp)))
            for t in range(c.NMT):
                for cc in range(2):
                    hd = 2 * t + cc
                    jobs.append((jb_q + 3 * c.NQT + t, cc, ones_bf[:, :], "ones_bf", 1.0 / 128, c.p_mq, U[:, qm0 + hd * TM: qm0 + (hd + 1) * TM], ("qm", hd)))
            pend = None
            cur_w = None
            for jn, (jt, cc, ones_ap, ones_key, inv_n, gcol, dst_ap, dst_key) in enumerate(jobs):
                if cc == 0:
                    cur_w = wtile(jt)
                wt, wk = cur_w
                b = bank()
                mm_group(ps[b][:, :TM], [(wt[:, kc * 256 + cc * 128: kc * 256 + (cc + 1) * 128], hc(kc)) for kc in range(KC)], hkeys + [wk], ("ps", b))
                qknorm_a(b, TM, jn % 2)
                if pend is not None:
                    qknorm_b(*pend)
                pend = (b, TM, jn % 2, ones_ap, ones_key, inv_n, gcol, dst_ap, dst_key)
            qknorm_b(*pend)
            j = jb_q + 2 * c.NQT
            for t in range(c.NQT):
                wt, wk = wtile(j)
                j += 1
                for tc in range(NQ):
                    b = bank()
                    mm_group(ps[b][:, :256], [(h[:, kc, c0 + tc * 128: c0 + (tc + 1) * 128], wt[:, kc * 256:(kc + 1) * 256]) for kc in range(KC)],
                             hkeys + [wk], ("ps", b))
                    P.op("act", lambda e, b=b, tc=tc, t=t: e.activation(out=Vc[:, (jq0 + tc) * c.FW + t * 256:(jq0 + tc) * c.FW + (t + 1) * 256], in_=ps[b][:, :256], func=AF.Copy),
                         reads=[("ps", b)], writes=[("Vc", jq0 + tc)])
            if t0 == 0:
                P.op("dve", lambda e: e.memset(carry[:, 0, :], 0.0), writes=[("carry", 0)])
            for tc in range(NQ):
                jj = jq0 + tc
                b = bank()
                mm_group(ps[b][:, :HF], [(h[:, kc, c0 + tc * 128: c0 + (tc + 1) * 128], wf_ap(kc)) for kc in range(KC)], hkeys + ["swb"], ("ps", b))
                P.op("dve", lambda e, b=b: e.tensor_tensor(out=zt[:, 0, :], in0=ps[b][:, :HF], in1=prm[:, c.p_bf:c.p_bf + HF], op=ALU.add),
                     reads=[("ps", b), "prm"], writes=["z0"])
                P.op("act", lambda e: e.activation(out=zt[:, 1, :], in_=zt[:, 0, :], func=AF.Exp, scale=-1.0), reads=["z0"], writes=["z1"])
                P.op("act", lambda e: e.activation(out=zt[:, 2, :], in_=zt[:, 1, :], func=AF.Ln, bias=1.0, scale=1.0), reads=["z1"], writes=["z2"])
                b1, b2 = bank(), bank()
                P.op("pe", lambda e, b1=b1: e.matmul(ps[b1][:, :HF], lhsT=tri[:, :], rhs=zt[:, 2, :], start=True, stop=True),
                     reads=["z2", "tri"], writes=[("ps", b1)])
                P.op("pe", lambda e, b2=b2: e.matmul(ps[b2][:, :HF], lhsT=ones_f[:, :], rhs=zt[:, 2, :], start=True, stop=True),
                     reads=["z2", "ones_f"], writes=[("ps", b2)])
                P.op("dve", lambda e, b1=b1, jj=jj: e.tensor_tensor(out=Cneg[:, jj, :], in0=ps[b1][:, :HF], in1=carry[:, jj, :], op=ALU.add),
                     reads=[("ps", b1), ("carry", jj)], writes=[("Cneg", jj)])
                P.op("dve", lambda e, b2=b2, jj=jj: e.tensor_tensor(out=carry[:, jj + 1, :], in0=ps[b2][:, :HF], in1=carry[:, jj, :], op=ALU.add),
                     reads=[("ps", b2), ("carry", jj)], writes=[("carry", jj + 1)])
            for jk in range(jq0 + NQ):
                P.op("dve", lambda e, jk=jk: e.tensor_tensor(out=biast[:, jk, :], in0=Cneg[:, jk, :], in1=carry[:, jq0 + 1, :], op=ALU.subtract),
                     reads=[("Cneg", jk), ("carry", jq0 + 1)], writes=[("bias", jk)])
            j = c.mix_base + c.NPT + 3 * c.NQT + c.NMT

            nk = jq0 + NQ
            assert NQ <= 2
            spool = (0, 1, 2, 3)
            its = [("f", hp, hh, jk) for hp in range(HP) for hh in range(2) for jk in range(nk)]
            its += [("m", hd, 0, mc) for hd in range(HM) for mc in range(MC)]

            def it_qk(n):
                kind, a_, hh, jk = its[n]
                bs = spool[n % 4]
                if kind == "f":
                    hp, r0 = a_, 64 * hh
                    cq = max(0, jk - jq0) * 128
                    P.op("pe", lambda e: e.matmul(ps[bs][:, cq:TM], lhsT=kT[r0:r0 + 64, hp, jk * 128:(jk + 1) * 128],
                                                  rhs=U[r0:r0 + 64, q0 + hp * TM + cq: q0 + (hp + 1) * TM], start=True, stop=True),
                         reads=[("kT", hp), ("q", hp)], writes=[("ps", bs)])
                else:
                    hd, mc = a_, jk
                    P.op("pe", lambda e: e.matmul(ps[bs][:, :TM], lhsT=km[:, hd, mc * 128:(mc + 1) * 128],
                                                  rhs=U[:, qm0 + hd * TM: qm0 + (hd + 1) * TM], start=True, stop=True),
                         reads=[("km", hd), ("qm", hd)], writes=[("ps", bs)])

            def it_rest(n):
                kind, a_, hh, jk = its[n]
                bs = spool[n % 4]
                pb = n % 3
                p0 = pt[pb]
                if kind == "f":
                    hp, r0 = a_, 64 * hh
                    hd = 2 * hp + hh
                    gi = hp
                    cq = max(0, jk - jq0) * 128
                    first, lastk = (jk == 0), (jk == nk - 1)
                    P.op("act", lambda e: e.activation(out=U[:, p0 + cq: p0 + TM], in_=ps[bs][:, cq:TM], func=AF.Exp,
                                                       bias=biast[:, jk, hd:hd + 1], scale=0.125),
                         reads=[("ps", bs), ("bias", jk)], writes=[("pt", pb)])
                    if jk >= jq0:
                        jb = jk - jq0
                        P.op("pool", lambda e: e.tensor_tensor(out=U[:, p0 + jb * 128: p0 + (jb + 1) * 128], in0=U[:, p0 + jb * 128: p0 + (jb + 1) * 128],
                                                               in1=tri_bf[:, :], op=ALU.mult),
                             reads=[("pt", pb), "tri_bf"], writes=[("pt", pb)])
                    vl, vkey, ol, rows = Vc[:, jk * c.FW + hd * 64: jk * c.FW + (hd + 1) * 64], ("Vc", jk), ones_bf[:, 0:64], slice(r0, r0 + 64)
                    fin = lastk and hh == 1
                    dst0, dkey = at0 + hp * TM, ("at", hp)
                else:
                    hd, mc = a_, jk
                    gi = HP + hd
                    cq = 0
                    first, lastk = (mc == 0), (mc == MC - 1)
                    P.op("act", lambda e: e.activation(out=U[:, p0: p0 + TM], in_=ps[bs][:, :TM], func=AF.Exp, scale=128.0 ** -0.5),
                         reads=[("ps", bs)], writes=[("pt", pb)])
                    vl, vkey, ol, rows = vm[:, mc, hd * 128:(hd + 1) * 128], ("vm", mc), ones_bf[:, :], slice(0, 128)
                    fin = lastk
                    dst0, dkey = mo0 + hd * TM, ("mo", hd)
                bn, bd = (4, 5) if gi % 2 == 0 else (6, 7)
                P.op("pe", lambda e: e.matmul(ps[bn][rows, cq:TM], lhsT=vl, rhs=U[:, p0 + cq: p0 + TM], start=first, stop=lastk),
                     reads=[("pt", pb), vkey], writes=[("ps", bn)], signal=False)
                P.op("pe", lambda e: e.matmul(ps[bd][rows, cq:TM], lhsT=ol, rhs=U[:, p0 + cq: p0 + TM], start=first, stop=lastk),
                     reads=[("pt", pb), "ones_bf"], writes=[("ps", bd)], signal=True)
                if fin:
                    r_ = rd[gi % 2]
                    P.op("dve", lambda e: e.reciprocal(out=U32[:, r_:r_ + TM], in_=ps[bd][:, :TM]), reads=[("ps", bd)], writes=[("rd", gi % 2)])
                    P.op("dve", lambda e: e.tensor_tensor(out=U[:, dst0: dst0 + TM], in0=ps[bn][:, :TM], in1=U32[:, r_:r_ + TM], op=ALU.mult),
                         reads=[("ps", bn), ("rd", gi % 2)], writes=[dkey])

            LA = 2
            for n in range(len(its) + LA):
                if n < len(its):
                    it_qk(n)
                if n >= LA:
                    it_rest(n - LA)

            for dc in range(KC):
                wa, wak = wtile(j)
                wb, wbk = wtile(j + 1)
                j += 2
                o = KC * 128
                branches = (
                    ([wa[:, kc * 128:(kc + 1) * 128] for kc in range(KC)], wak,
                     [(wb[:, o + kk * 128: o + (kk + 1) * 128], U[:, mx0 + kk * TM: mx0 + (kk + 1) * TM]) for kk in range(4)], [("mx", g) for g in range(4)]),
                    ([wa[:, (KC + kc) * 128:(KC + kc + 1) * 128] for kc in range(KC)], wak,
                     [(wb[:, o + (4 + kk) * 128: o + (5 + kk) * 128], U[:, at0 + kk * TM: at0 + (kk + 1) * TM]) for kk in range(HP)], [("at", g) for g in range(HP)]),
                    ([wb[:, kc * 128:(kc + 1) * 128] for kc in range(KC)], wbk,
                     [(wb[:, o + (4 + HP + kk) * 128: o + (5 + HP + kk) * 128], U[:, mo0 + kk * TM: mo0 + (kk + 1) * TM]) for kk in range(HM)], [("mo", g) for g in range(HM)]),
                )
                for bi, (gws, gk, ypairs, ykeys) in enumerate(branches):
                    bg, by = bank(), bank()
                    mm_group(ps[bg][:, :TM], [(gws[kc], hc(kc)) for kc in range(KC)], hkeys + [gk], ("ps", bg))
                    mm_group(ps[by][:, :TM], ypairs, ykeys + [wbk], ("ps", by))
                    P.op("act", lambda e, bg=bg, bi=bi: e.activation(out=U32[:, sgt[bi]: sgt[bi] + TM], in_=ps[bg][:, :TM], func=AF.Sigmoid),
                         reads=[("ps", bg)], writes=[("sgt", bi)])
                    P.op("dve", lambda e, by=by, bi=bi: e.tensor_tensor(out=U32[:, tt[bi]: tt[bi] + TM], in0=ps[by][:, :TM], in1=U32[:, sgt[bi]: sgt[bi] + TM], op=ALU.mult),
                         reads=[("ps", by), ("sgt", bi)], writes=[("tt", bi)])
                P.op("pool", lambda e: e.tensor_tensor(out=U32[:, tt[0]: tt[0] + TM], in0=U32[:, tt[0]: tt[0] + TM], in1=U32[:, tt[1]: tt[1] + TM], op=ALU.add),
                     reads=[("tt", 0), ("tt", 1)], writes=[("tt", 0)])
                P.op("pool", lambda e, dc=dc: e.tensor_tensor(out=U[:, mg0 + dc * TM: mg0 + (dc + 1) * TM], in0=U32[:, tt[0]: tt[0] + TM], in1=U32[:, tt[2]: tt[2] + TM], op=ALU.add),
                     reads=[("tt", 0), ("tt", 2)], writes=[("mg", dc)])
            mgk = [("mg", dc) for dc in range(KC)]
            for t in range(KC // 2):
                wt, wk = wtile(j)
                j += 1
                for cc in range(2):
                    dc = 2 * t + cc
                    b = bank()
                    mm_group(ps[b][:, :TM], [(wt[:, kc * 256 + cc * 128: kc * 256 + (cc + 1) * 128], U[:, mg0 + kc * TM: mg0 + (kc + 1) * TM]) for kc in range(KC)],
                             mgk + [wk], ("ps", b))
                    P.op("dve", lambda e, b=b, dc=dc: e.tensor_tensor(out=x[:, dc, c0:c0 + TM], in0=ps[b][:, :TM], in1=x[:, dc, c0:c0 + TM], op=ALU.add),
                         reads=[("ps", b), ("x", dc)], writes=[("x", dc)])
            assert j == c.ffn_base[2]
            P.barrier()

        xkeys = [("x", kc) for kc in range(KC)]
        step = max(1, KC // 4)
        last = None
        for s in range(c.NSEQ):
            memkv(s)
            for i in range(S // T):
                toks = []
                for k0 in range(0, KC, step):
                    toks.append(P.op("act", lambda e, k0=k0, s=s, i=i: e.dma_start(
                        out=x[:, k0:k0 + step, :], in_=xd[s, k0:k0 + step, :, i * T:(i + 1) * T].rearrange("k p t -> p k t")),
                        writes=xkeys[k0:k0 + step], dma="xld"))
                P.retoken([], xkeys, toks, toks[-1])
                ffn(1, c.p_g1)
                rmsnorm_stream(lambda kc: x[:, kc, :], xkeys, lambda kc: h[:, kc, :], [("h", kc) for kc in range(KC)], c.p_gmix, KC, T, 1.0 / c.D)
                for c0 in range(0, T, TM):
                    mixer(s, i, c0)
                ffn(2, c.p_g2)
                toks = []
                for k0 in range(0, KC, step):
                    toks.append(P.op("act", lambda e, k0=k0, s=s, i=i: e.dma_start(
                        out=od[s, k0:k0 + step, :, i * T:(i + 1) * T].rearrange("k p t -> p k t"), in_=x[:, k0:k0 + step, :]),
                        reads=xkeys[k0:k0 + step], dma="xst"))
                P.retoken(xkeys, [], toks, toks[-1])
                last = toks[-1]
        flush_store()
        P.wait_all("act", [last])
        P.emit()
    return nc


_CACHE = {}


def kernel(**inputs):
    cfg = Cfg()
    inp = {k: np.asarray(v) for k, v in inputs.items()}
    ncores = 8
    B = inp["x"].shape[0]
    assert B == ncores * cfg.NSEQ
    wts, sw, prm = pack_weights(inp, cfg)
    x = inp["x"].astype(np.float32, copy=False)
    mem = inp["mem"].astype(np.float32, copy=False)
    in_maps = []
    for r in range(ncores):
        xs = x[r * cfg.NSEQ:(r + 1) * cfg.NSEQ]
        xT = np.ascontiguousarray(xs.transpose(0, 2, 1)).reshape(cfg.NSEQ, cfg.KC, 128, cfg.S)
        ms = mem[r * cfg.NSEQ:(r + 1) * cfg.NSEQ]
        mT = np.ascontiguousarray(ms.transpose(0, 2, 1)).reshape(cfg.NSEQ, cfg.KC, 128, cfg.M)
        in_maps.append({"xT": xT, "memT": mT, "wts": wts, "sw": sw, "prm": prm})
    if "nc" not in _CACHE:
        _CACHE["nc"] = build_program(cfg)
    res = run_bass_kernel_spmd(_CACHE["nc"], in_maps, core_ids=list(range(ncores)))
    out = np.empty((B, cfg.S, cfg.D), np.float32)
    for r in range(ncores):
        oT = np.asarray(res.results[r]["outT"]).reshape(cfg.NSEQ, cfg.D, cfg.S)
        out[r * cfg.NSEQ:(r + 1) * cfg.NSEQ] = oT.transpose(0, 2, 1)
    return out
```
